# Optimizing a Trainium2 kernel written in Bass

```python
import jax, jax.numpy as jnp
from jax import lax
import numpy as np

D_MODEL = 1024
BATCH = 16
SEQ = 256
DEPTH = 2
DEC_BATCH = 2
DEC_SEQ = 1024
PAST_LEN = 512

GRID_W = 64
MIX_W = D_MODEL
H_GLA = 4
DK_GLA = 64
DV_GLA = 128
GLA_RANK = 16
GATE_NORM = 16.0
GLA_CHUNK = 32
H_NAT = 8
HD_NAT = 64
WIN_H = 8
WIN_W = 16
Q_BLOCK_W = 16
BAND_W = Q_BLOCK_W + WIN_W
D_FF = 2816
N_MOD = 9
ROPE_BASE = 10000.0
EPS = 1e-6
NEG_INF = -1e30
ATTN_QBLOCK = 128
D_IN = 2 * H_GLA * DK_GLA + 2 * H_GLA * DV_GLA + 2 * GLA_RANK + 3 * H_NAT * HD_NAT

kernel_name = 'hymba_gla_natten_macaron_prefix_dit_step'


def rmsnorm(x, g):
    xf = x.astype(jnp.float32)
    y = xf * lax.rsqrt(jnp.mean(xf * xf, axis=-1, keepdims=True) + EPS)
    return (y * g.astype(jnp.float32)).astype(x.dtype)


def modulation(cvec, w_mod, b_mod):
    m = jax.nn.silu(cvec) @ w_mod + b_mod
    return m.reshape(cvec.shape[0], N_MOD, 1, D_MODEL)


def swiglu(h, w_in, w_out):
    gate, up = jnp.split(h @ w_in, 2, axis=-1)
    return (jax.nn.silu(gate) * up) @ w_out


def rope_axis(x, pos):
    half = x.shape[-1] // 2
    freqs = ROPE_BASE ** (-jnp.arange(half, dtype=jnp.float32) / half)
    ang = pos.astype(jnp.float32)[:, None] * freqs[None, :]
    cos = jnp.cos(ang)[:, None, :]
    sin = jnp.sin(ang)[:, None, :]
    xf = x.astype(jnp.float32)
    x1, x2 = xf[..., :half], xf[..., half:]
    return jnp.concatenate([x1 * cos - x2 * sin, x1 * sin + x2 * cos], axis=-1).astype(x.dtype)


def rope_2d(x):
    t = jnp.arange(x.shape[1])
    half = x.shape[-1] // 2
    return jnp.concatenate([rope_axis(x[..., :half], t // GRID_W),
                            rope_axis(x[..., half:], t % GRID_W)], axis=-1)


def gla_chunked(q, k, v, g, s0):
    f32 = jnp.float32
    b_, t_, h_, dk = q.shape
    dv = v.shape[-1]
    n = t_ // GLA_CHUNK
    qc = q.astype(f32).reshape(b_, n, GLA_CHUNK, h_, dk)
    kc = k.astype(f32).reshape(b_, n, GLA_CHUNK, h_, dk)
    vc = v.astype(f32).reshape(b_, n, GLA_CHUNK, h_, dv)
    cum = jnp.cumsum(g.astype(f32).reshape(b_, n, GLA_CHUNK, h_, dk), axis=2)
    last = cum[:, :, -1]
    causal = jnp.tril(jnp.ones((GLA_CHUNK, GLA_CHUNK), dtype=bool))[None, None, :, :, None, None]
    diff = cum[:, :, :, None] - cum[:, :, None, :]
    decay = jnp.where(causal, jnp.exp(jnp.where(causal, diff, 0.0)), 0.0)
    scores = jnp.einsum('bnthd,bnshd,bntshd->bnhts', qc, kc, decay)
    o_intra = jnp.einsum('bnhts,bnshv->bnthv', scores, vc)
    kv = jnp.einsum('bnshd,bnshv->bnhdv', kc * jnp.exp(last[:, :, None] - cum), vc)
    chunk_decay = jnp.exp(last)

    def step(state, inp):
        d, kv_c = inp
        return d[..., None] * state + kv_c, state

    s_final, s_enter = lax.scan(step, s0.astype(f32),
                                (chunk_decay.swapaxes(0, 1), kv.swapaxes(0, 1)))
    o_inter = jnp.einsum('bnthd,nbhdv->bnthv', qc * jnp.exp(cum), s_enter)
    return (o_intra + o_inter).reshape(b_, t_, h_, dv), s_final


def gla_bidir(q, k, v, g_f, g_b, s0_f, s0_b):
    flip = lambda a: jnp.flip(a, axis=1)
    o_f, s_f = gla_chunked(q, k, v, g_f, s0_f)
    o_b, s_b = gla_chunked(flip(q), flip(k), flip(v), flip(g_b), s0_b)
    return o_f + flip(o_b), s_f, s_b


def project(h, w_in, gla_wa2, gla_ba):
    b_, t_, _ = h.shape
    sizes = [H_GLA * DK_GLA, H_GLA * DK_GLA, H_GLA * DV_GLA, H_GLA * DV_GLA, GLA_RANK, GLA_RANK,
             H_NAT * HD_NAT, H_NAT * HD_NAT, H_NAT * HD_NAT]
    cuts = [int(c) for c in np.cumsum(sizes)[:-1]]
    q_g, k_g, v_g, r_g, lr_f, lr_b, q_n, k_n, v_n = jnp.split(h @ w_in, cuts, axis=-1)

    def gate(lr, i):
        z = (lr @ gla_wa2[i] + gla_ba[i]).astype(jnp.float32)
        return (jax.nn.log_sigmoid(z) / GATE_NORM).reshape(b_, t_, H_GLA, DK_GLA)

    heads = lambda a, nh: a.reshape(b_, t_, nh, -1)
    return (heads(q_g, H_GLA) * DK_GLA ** -0.5, heads(k_g, H_GLA), heads(v_g, H_GLA), r_g,
            gate(lr_f, 0), gate(lr_b, 1), heads(q_n, H_NAT), heads(k_n, H_NAT), heads(v_n, H_NAT))


def gla_merge(o, r, g_norm):
    b_, t_ = r.shape[:2]
    o = rmsnorm(o, g_norm).reshape(b_, t_, -1)
    return (o * jax.nn.silu(r.astype(jnp.float32))).astype(r.dtype)


def ctx_attention(q, k, v):
    b_, s_, h_, hd = q.shape
    qb = q.reshape(b_, s_ // ATTN_QBLOCK, ATTN_QBLOCK, h_, hd).swapaxes(0, 1)

    def block(qi):
        logits = jnp.einsum('bqhd,bkhd->bhqk', qi, k).astype(jnp.float32) * hd ** -0.5
        p = jax.nn.softmax(logits, axis=-1).astype(v.dtype)
        return jnp.einsum('bhqk,bkhd->bqhd', p, v)

    o = lax.map(block, qb)
    return o.swapaxes(0, 1).reshape(b_, s_, h_ * hd)


def natten_latent(q, k, v, ck, cv, rpb):
    f32 = jnp.float32
    b_, n_, h_, hd = q.shape
    rows = n_ // GRID_W
    wh = min(WIN_H, rows)
    n_cb = GRID_W // Q_BLOCK_W
    qr = jnp.arange(rows)
    key_rows = jnp.clip(qr - wh // 2, 0, rows - wh)[:, None] + jnp.arange(wh)[None, :]
    band_start = jnp.clip(jnp.arange(n_cb) * Q_BLOCK_W - WIN_W // 2, 0, GRID_W - BAND_W)
    key_cols = band_start[:, None] + jnp.arange(BAND_W)[None, :]
    qcol = jnp.arange(GRID_W).reshape(n_cb, Q_BLOCK_W)
    win_start = jnp.clip(qcol - WIN_W // 2, 0, GRID_W - WIN_W)
    rel = key_cols[:, None, :] - win_start[:, :, None]
    col_valid = (rel >= 0) & (rel < WIN_W)
    dr_idx = key_rows - qr[:, None] + (WIN_H - 1)
    dc_idx = jnp.clip(key_cols[:, None, :] - qcol[:, :, None] + (WIN_W - 1), 0, 2 * WIN_W - 2)
    bias = rpb[:, dr_idx[:, None, None, :, None], dc_idx[None, :, :, None, :]].astype(f32)
    kg = k.reshape(b_, rows, GRID_W, h_, hd)
    vg = v.reshape(b_, rows, GRID_W, h_, hd)
    k_band = kg[:, key_rows[:, :, None, None], key_cols[None, None, :, :]]
    v_band = vg[:, key_rows[:, :, None, None], key_cols[None, None, :, :]]
    q_blk = q.reshape(b_, rows, n_cb, Q_BLOCK_W, h_, hd)
    scale = hd ** -0.5
    s_win = jnp.einsum('brjihd,brajmhd->bhrjiam', q_blk, k_band).astype(f32) * scale + bias
    s_win = jnp.where(col_valid[:, :, None, :], s_win, NEG_INF)
    s_ctx = jnp.einsum('brjihd,bphd->bhrjip', q_blk, ck).astype(f32) * scale
    n_win = wh * BAND_W
    logits = jnp.concatenate([s_win.reshape(b_, h_, rows, n_cb, Q_BLOCK_W, n_win), s_ctx], axis=-1)
    p = jax.nn.softmax(logits, axis=-1).astype(v.dtype)
    p_win = p[..., :n_win].reshape(b_, h_, rows, n_cb, Q_BLOCK_W, wh, BAND_W)
    p_ctx = p[..., n_win:]
    o = (jnp.einsum('bhrjiam,brajmhd->brjihd', p_win, v_band)
         + jnp.einsum('bhrjip,bphd->brjihd', p_ctx, cv))
    return o.reshape(b_, n_, h_ * hd)


def mixer_context(h, w_in, gla_wa2, gla_ba, gla_norm_g, w_out):
    q_g, k_g, v_g, r_g, g_f, g_b, q_n, k_n, v_n = project(h, w_in, gla_wa2, gla_ba)
    zeros = jnp.zeros((h.shape[0], H_GLA, DK_GLA, DV_GLA), jnp.float32)
    o_g, s_f, s_b = gla_bidir(q_g, k_g, v_g, g_f, g_b, zeros, zeros)
    o_n = ctx_attention(q_n, k_n, v_n)
    y = jnp.concatenate([gla_merge(o_g, r_g, gla_norm_g), o_n], axis=-1) @ w_out
    state = jnp.stack([s_f, s_b], axis=1).astype(h.dtype)
    return y, (k_n, v_n, state)


def mixer_latent(h, cache_k, cache_v, state, w_in, gla_wa2, gla_ba, gla_norm_g, nat_rpb, w_out):
    q_g, k_g, v_g, r_g, g_f, g_b, q_n, k_n, v_n = project(h, w_in, gla_wa2, gla_ba)
    q_g = rope_2d(q_g)
    k_g = rope_2d(k_g)
    o_g, _, _ = gla_bidir(q_g, k_g, v_g, g_f, g_b, state[:, 0], state[:, 1])
    o_n = natten_latent(q_n, k_n, v_n, cache_k, cache_v, nat_rpb)
    y = jnp.concatenate([gla_merge(o_g, r_g, gla_norm_g), o_n], axis=-1) @ w_out
    return y, None


def macaron_layer(x, mod, norm_g, ffn_w_in, ffn_w_out, mixer):
    h = rmsnorm(x, norm_g[0]) * (1.0 + mod[:, 1]) + mod[:, 0]
    x = x + 0.5 * mod[:, 2] * rmsnorm(swiglu(h, ffn_w_in[0], ffn_w_out[0]), norm_g[1])
    h = rmsnorm(x, norm_g[2]) * (1.0 + mod[:, 4]) + mod[:, 3]
    y, extras = mixer(h)
    x = x + mod[:, 5] * rmsnorm(y, norm_g[3])
    h = rmsnorm(x, norm_g[4]) * (1.0 + mod[:, 7]) + mod[:, 6]
    x = x + 0.5 * mod[:, 8] * rmsnorm(swiglu(h, ffn_w_in[1], ffn_w_out[1]), norm_g[5])
    return x, extras


def setup_inputs(seed: int = 0) -> dict:
    key = jax.random.key(seed)
    ks = jax.random.split(key, 18)
    nrm = lambda k, shape, s=1.0: s * jax.random.normal(k, shape, jnp.float32)
    return {
        'x_prompt': nrm(ks[0], (BATCH, SEQ, D_MODEL)),
        'x_sample': nrm(ks[1], (DEC_BATCH, DEC_SEQ, D_MODEL)),
        'cache_k': nrm(ks[2], (DEC_BATCH, DEPTH, PAST_LEN, H_NAT, HD_NAT)),
        'cache_v': nrm(ks[3], (DEC_BATCH, DEPTH, PAST_LEN, H_NAT, HD_NAT)),
        'state_gla': nrm(ks[4], (DEC_BATCH, DEPTH, 2, H_GLA, DK_GLA, DV_GLA), 0.5),
        'c': nrm(ks[5], (DEC_BATCH, D_MODEL)),
        'c_ctx': nrm(ks[6], (D_MODEL,)),
        'w_mod': nrm(ks[7], (DEPTH, D_MODEL, N_MOD * D_MODEL), 0.5 * D_MODEL ** -0.5),
        'b_mod': nrm(ks[8], (DEPTH, N_MOD * D_MODEL), 0.02),
        'norm_g': 1.0 + nrm(ks[9], (DEPTH, 6, D_MODEL), 0.05),
        'ffn_w_in': nrm(ks[10], (DEPTH, 2, D_MODEL, 2 * D_FF), D_MODEL ** -0.5),
        'ffn_w_out': nrm(ks[11], (DEPTH, 2, D_FF, D_MODEL), D_FF ** -0.5),
        'w_in': nrm(ks[12], (DEPTH, D_MODEL, D_IN), D_MODEL ** -0.5),
        'gla_wa2': nrm(ks[13], (DEPTH, 2, GLA_RANK, H_GLA * DK_GLA), GLA_RANK ** -0.5),
        'gla_ba': nrm(ks[14], (DEPTH, 2, H_GLA * DK_GLA), 0.1),
        'gla_norm_g': 1.0 + nrm(ks[15], (DEPTH, DV_GLA), 0.05),
        'nat_rpb': nrm(ks[16], (DEPTH, H_NAT, 2 * WIN_H - 1, 2 * WIN_W - 1), 0.1),
        'w_out': nrm(ks[17], (DEPTH, MIX_W, D_MODEL), MIX_W ** -0.5),
    }


def reference(x_prompt, x_sample, cache_k, cache_v, state_gla, c, c_ctx, w_mod, b_mod, norm_g,
              ffn_w_in, ffn_w_out, w_in, gla_wa2, gla_ba, gla_norm_g, nat_rpb, w_out):
    xp = x_prompt
    k_list, v_list, s_list = [], [], []
    for l in range(DEPTH):
        mod = modulation(c_ctx[None, :], w_mod[l], b_mod[l])
        mixer = lambda h, l=l: mixer_context(h, w_in[l], gla_wa2[l], gla_ba[l], gla_norm_g[l], w_out[l])
        xp, (k_l, v_l, s_l) = macaron_layer(xp, mod, norm_g[l], ffn_w_in[l], ffn_w_out[l], mixer)
        k_list.append(k_l)
        v_list.append(v_l)
        s_list.append(s_l)
    new_cache_k = jnp.stack(k_list, axis=1)
    new_cache_v = jnp.stack(v_list, axis=1)
    new_state_gla = jnp.stack(s_list, axis=1)

    xs = x_sample
    for l in range(DEPTH):
        mod = modulation(c, w_mod[l], b_mod[l])
        mixer = lambda h, l=l: mixer_latent(h, cache_k[:, l], cache_v[:, l], state_gla[:, l], w_in[l],
                                            gla_wa2[l], gla_ba[l], gla_norm_g[l], nat_rpb[l], w_out[l])
        xs, _ = macaron_layer(xs, mod, norm_g[l], ffn_w_in[l], ffn_w_out[l], mixer)

    return (xp, xs, new_cache_k, new_cache_v, new_state_gla)
```

```python
from collections import deque
from contextlib import ExitStack

import os
import numpy as np
import concourse.bass as bass
import concourse.mybir as mybir
from concourse.bass_utils import run_bass_kernel_spmd

F32 = mybir.dt.float32
BF16 = mybir.dt.bfloat16
AF = mybir.ActivationFunctionType
ALU = mybir.AluOpType

D = 1024
KC = 8
DFF = 2816
FC = 22
DIN = 3104
EPS = 1e-6
NEG = -30000.0
NW = 4
SLOT = 4096


class Buf:
    __slots__ = ("name", "w", "r")

    def __init__(self, name=""):
        self.name = name
        self.w = None
        self.r = []


class V:
    __slots__ = ("ap", "bufs")

    def __init__(self, ap, bufs):
        self.ap = ap
        self.bufs = bufs if isinstance(bufs, (list, tuple)) else [bufs]

    def __getitem__(self, idx):
        return V(self.ap[idx], self.bufs)

    def re(self, pat, **kw):
        return V(self.ap.rearrange(pat, **kw), self.bufs)


class TK:
    NDMA = 12

    def __init__(self, nc, es):
        self.nc = nc
        self.es = es
        self.eng = {"pe": nc.tensor, "act": nc.scalar, "dve": nc.vector, "pool": nc.gpsimd, "sp": nc.sync}
        self.semobj = {}
        self.cnt = {}
        self.waited = {e: {} for e in self.eng}
        for e in self.eng:
            self.semobj[e] = es.enter_context(nc.semaphore("s_" + e))
            self.cnt[e] = 0
        self.dma_idx = {}
        for q in ("sp", "pool"):
            self.dma_idx[q] = 0
            for i in range(self.NDMA):
                nm = f"d_{q}_{i}"
                self.semobj[nm] = es.enter_context(nc.semaphore(nm))
                self.cnt[nm] = 0
        self.pe_open = False
        self.nins = 0
        self.nwait = 0

    def sb(self, es, name, shape, dt=F32):
        self.nalloc = getattr(self, "nalloc", 0) + 1
        name = f"{name}_{self.nalloc}"
        t = es.enter_context(self.nc.sbuf_tensor(name, list(shape), dt))
        return V(t[:], Buf(name))

    def ps(self, es, name, shape, dt=F32):
        t = es.enter_context(self.nc.psum_tensor(name, list(shape), dt))
        return V(t[:], Buf(name))

    def _wait(self, e, sk, v):
        if self.waited[e].get(sk, 0) >= v:
            return
        self.eng[e].wait_ge(self.semobj[sk], v)
        self.waited[e][sk] = v
        self.nwait += 1

    def _deps(self, e, reads, writes):
        need = {}

        def add(idn):
            if idn is None:
                return
            sk, v = idn
            if need.get(sk, 0) < v:
                need[sk] = v

        for b in reads:
            add(b.w)
        for b in writes:
            add(b.w)
            for r in b.r:
                add(r)
        for sk, v in need.items():
            if sk == e:
                continue
            self._wait(e, sk, v)

    def _update(self, ident, reads, writes):
        for b in reads:
            b.r = [r for r in b.r if r[0] != ident[0]]
            b.r.append(ident)
        for b in writes:
            b.w = ident
            b.r = []

    def op(self, e, fn, reads, writes, inc=True):
        if e != "pe":
            assert not self.pe_open, "non-PE op emitted inside open PE group"
        rb = [b for v in reads for b in v.bufs]
        wb = [b for v in writes for b in v.bufs]
        self._deps(e, rb, wb)
        ins = fn()
        self.nins += 1
        if inc:
            self.cnt[e] += 1
            ins.then_inc(self.semobj[e], 1)
            ident = (e, self.cnt[e])
            if e == "pe":
                self.pe_open = False
        else:
            assert e == "pe"
            ident = (e, self.cnt[e] + 1)
            self.pe_open = True
        self._update(ident, rb, wb)
        return ins

    def dma(self, q, out, in_, reads=(), writes=(), **kw):
        assert not self.pe_open
        rb = [b for v in reads for b in v.bufs]
        wb = [b for v in writes for b in v.bufs]
        self._deps(q, rb, wb)
        i = self.dma_idx[q] % self.NDMA
        self.dma_idx[q] += 1
        nm = f"d_{q}_{i}"
        if self.cnt[nm]:
            self._wait(q, nm, self.cnt[nm])
        ins = self.eng[q].dma_start(out=out, in_=in_, **kw)
        self.cnt[nm] += 16
        ins.then_inc(self.semobj[nm], 16)
        self.nins += 1
        self._update((nm, self.cnt[nm]), rb, wb)
        return ins

    def barrier(self, engines=("pe", "act", "dve", "pool", "sp")):
        assert not self.pe_open
        for e in engines:
            for nm, c in self.cnt.items():
                if c and nm != e:
                    self._wait(e, nm, c)

    def mm(self, out, lhsT, rhs, start=True, stop=True, inc=None):
        if inc is None:
            inc = stop
        return self.op("pe", lambda: self.nc.tensor.matmul(out.ap, lhsT.ap, rhs.ap, start=start, stop=stop),
                       [lhsT, rhs], [out], inc=inc)

    def transpose(self, out, in_, ident):
        return self.op("pe", lambda: self.nc.tensor.transpose(out.ap, in_.ap, ident.ap), [in_, ident], [out])

    def act(self, out, in_, func, bias=None, scale=1.0):
        reads = [in_]
        kw = {}
        if bias is not None:
            if isinstance(bias, V):
                reads.append(bias)
                kw["bias"] = bias.ap
            else:
                kw["bias"] = bias
        if isinstance(scale, V):
            reads.append(scale)
            kw["scale"] = scale.ap
        else:
            kw["scale"] = scale
        return self.op("act", lambda: self.nc.scalar.activation(out=out.ap, in_=in_.ap, func=func, **kw),
                       reads, [out])

    def _e(self, e):
        return self.nc.vector if e == "dve" else self.nc.gpsimd

    def tt(self, e, out, a, b, op):
        return self.op(e, lambda: self._e(e).tensor_tensor(out.ap, a.ap, b.ap, op=op), [a, b], [out])

    def ts(self, e, out, a, s1, s2=None, op0=ALU.mult, op1=None):
        reads = [a]
        s1a = s1.ap if isinstance(s1, V) else s1
        s2a = s2.ap if isinstance(s2, V) else s2
        if isinstance(s1, V):
            reads.append(s1)
        if isinstance(s2, V):
            reads.append(s2)
        kw = {}
        if op1 is not None:
            kw["op1"] = op1
        return self.op(e, lambda: self._e(e).tensor_scalar(out.ap, a.ap, s1a, s2a, op0, **kw), reads, [out])

    def stt(self, e, out, in0, scalar, in1, op0, op1):
        reads = [in0, in1]
        sa = scalar.ap if isinstance(scalar, V) else scalar
        if isinstance(scalar, V):
            reads.append(scalar)
        return self.op(e, lambda: self._e(e).scalar_tensor_tensor(out.ap, in0.ap, sa, in1.ap, op0, op1),
                       reads, [out])

    def copy(self, e, out, in_):
        if e == "act":
            return self.op("act", lambda: self.nc.scalar.copy(out.ap, in_.ap), [in_], [out])
        return self.op(e, lambda: self._e(e).tensor_copy(out.ap, in_.ap), [in_], [out])

    def memset(self, e, out, val):
        return self.op(e, lambda: self._e(e).memset(out.ap, val), [], [out])

    def recip(self, out, in_):
        return self.op("dve", lambda: self.nc.vector.reciprocal(out.ap, in_.ap), [in_], [out])


def make_consts():
    c = {}
    c["identf"] = np.eye(128, dtype=np.float32)
    s = np.arange(128)[:, None]
    t = np.arange(128)[None, :]
    g = -1.0 / 16.0
    c["lmat"] = np.stack([(s <= t) * g, (s >= t) * g]).astype(np.float32)
    c["rmat"] = np.stack([(s > t) * g, (s < t) * g]).astype(np.float32)
    c["mask"] = np.stack([(s <= t) * 1.0, (s >= t) * 1.0]).astype(np.float32)
    rm = np.zeros((128, 128), np.float32)
    for p in range(128):
        i = p % 32
        if i < 16:
            rm[p + 16, p] = -1.0
        else:
            rm[p - 16, p] = 1.0
    c["ropem"] = rm
    tt_ = np.arange(1024)
    cos = np.zeros((128, 1024), np.float64)
    sin = np.zeros((128, 1024), np.float64)
    for p in range(128):
        d = p % 64
        blk = d // 32
        fi = (d % 32) % 16
        freq = np.float32(10000.0) ** (-np.float32(fi) / np.float32(16))
        pos = (tt_ // 64) if blk == 0 else (tt_ % 64)
        ang = pos.astype(np.float32) * np.float32(freq)
        cos[p] = np.cos(ang.astype(np.float64))
        sin[p] = np.sin(ang.astype(np.float64))
    c["ropecos"] = cos.astype(np.float32)
    c["ropesin"] = sin.astype(np.float32)
    ohk = np.zeros((128, 8, 128), np.float32)
    for kt in range(8):
        for i in range(2):
            for kc in range(64):
                ohk[2 * kt + i, kt, i * 64 + kc] = 1.0
                ohk[16 + kc, kt, i * 64 + kc] = 1.0
    pen = np.zeros((128, 1024), np.float32)
    for q in range(1024):
        qr, qc = q // 64, q % 64
        lo = min(max(qr - 4, 0), 8)
        ws = min(max(qc - 8, 0), 48)
        for r in range(16):
            if not (lo <= r < lo + 8):
                pen[r, q] = NEG
        for cc in range(64):
            if not (ws <= cc < ws + 16):
                pen[16 + cc, q] = NEG
    c["ohk"] = ohk.reshape(128, 1024)
    c["pen"] = pen
    c["j64"] = np.fliplr(np.eye(64, dtype=np.float32)).copy()
    c["zeros"] = np.zeros((136, 128), np.float32)
    return c


CONST_SHAPES = {"identf": [128, 128], "lmat": [2, 128, 128], "rmat": [2, 128, 128], "mask": [2, 128, 128],
                "ropem": [128, 128], "ropecos": [128, 1024], "ropesin": [128, 1024], "ohk": [128, 1024],
                "pen": [128, 1024], "j64": [64, 64], "zeros": [136, 128]}


class Kern:
    def __init__(self, nc):
        self.nc = nc
        self.kstop = float(os.environ.get("KSTOP", "9999"))
        self.stopped = False

    def ck(self, n):
        if n >= self.kstop:
            self.stopped = True
        return self.stopped

    def din(self, name, shape):
        return self.nc.dram_tensor(name, list(shape), F32, kind="ExternalInput").ap()

    def dout(self, name, shape):
        return self.nc.dram_tensor(name, list(shape), F32, kind="ExternalOutput").ap()

    def bank(self):
        b = self.bfree.popleft()
        self.bfree.append(b)
        return self.banks[b]

    def hold(self):
        b = self.bfree.popleft()
        return self.banks[b]

    def release(self, v):
        self.bfree.append([i for i in range(8) if self.banks[i] is v][0])

    def plan_weights(self):
        d = self.d
        plan = []

        def w2(ap2, c0, n):
            return ap2.rearrange("(k p) c -> p k c", p=128)[:, :, c0:c0 + n]

        for l in range(2):
            for b in range(18):
                plan.append((("mod", l, b), [(0, 8, 512, 0, 512, w2(d["w_mod"][l], b * 512, 512))]))
        for g in range(2):
            for l in range(2):
                def ffn(f):
                    for b in range(11):
                        plan.append((("fin", g, l, f, b), [
                            (0, 8, 512, 0, 256, w2(d["ffn_w_in"][l, f], b * 256, 256)),
                            (0, 8, 512, 256, 256, w2(d["ffn_w_in"][l, f], DFF + b * 256, 256))]))
                    for mp in range(4):
                        for kh in range(2):
                            src = d["ffn_w_out"][l, f].rearrange("(k p) c -> p k c", p=128)[
                                :, kh * 11:(kh + 1) * 11, mp * 256:(mp + 1) * 256]
                            plan.append((("fout", g, l, f, mp, kh), [(0, 11, 256, 0, 256, src)]))
                ffn(0)
                wi = d["w_in"][l]
                for bi, (c0, n) in enumerate([(0, 512), (512, 512), (1024, 512), (1536, 32), (1568, 512),
                                              (2080, 512), (2592, 512)]):
                    plan.append((("win", g, l, bi), [(0, 8, 512, 0, n, w2(wi, c0, n))]))
                for b in range(2):
                    plan.append((("wout", g, l, b), [(0, 8, 512, 0, 512, w2(d["w_out"][l], b * 512, 512))]))
                ffn(1)
        self.plan = plan
        self.ci = 0
        self.di = 0

    def _emit_w(self, j):
        tag, dmas = self.plan[j]
        slot = self.slots[j % NW]
        for di, (_, k, c, c0, n, src) in enumerate(dmas):
            dst = slot.ap[:, 0:k * c].rearrange("p (k c) -> p k c", k=k)[:, :, c0:c0 + n]
            wr = slot if len(dmas) == 1 else V(slot.ap, [slot.bufs[di]])
            self.tk.dma("pool", dst, src, writes=[wr])

    def wnext(self, tag):
        i = self.ci
        self.ci += 1
        while self.di < min(len(self.plan), i + NW):
            self._emit_w(self.di)
            self.di += 1
        assert self.plan[i][0] == tag, (self.plan[i][0], tag)
        k, c = self.plan[i][1][0][1], self.plan[i][1][0][2]
        slot = self.slots[i % NW]
        return V(slot.ap[:, 0:k * c].rearrange("p (k c) -> p k c", k=k), slot.bufs)

    def build(self):
        nc = self.nc
        d = {}
        d["xp"] = self.din("xp", [512, D])
        d["xs"] = self.din("xs", [1024, D])
        d["ck"] = self.din("ck", [2, 512, 512])
        d["cv"] = self.din("cv", [2, 512, 512])
        d["sg"] = self.din("sg", [2, 2, 256, 128])
        d["cvec"] = self.din("cvec", [16, 128])
        d["sel"] = self.din("sel", [128, 4])
        d["pen_own"] = self.din("pen_own", [128, 256])
        d["w_mod"] = self.din("w_mod", [2, D, 9 * D])
        d["b_mod"] = self.din("b_mod", [2, 72, 128])
        d["norm_g"] = self.din("norm_g", [96, 128])
        d["ffn_w_in"] = self.din("ffn_w_in", [2, 2, D, 2 * DFF])
        d["ffn_w_out"] = self.din("ffn_w_out", [2, 2, DFF, D])
        d["w_in"] = self.din("w_in", [2, D, DIN])
        d["gla_wa2"] = self.din("gla_wa2", [2, 2, 16, 256])
        d["gla_ba"] = self.din("gla_ba", [2, 2, 1, 256])
        d["gla_norm_g"] = self.din("gla_norm_g", [2, 128])
        d["nat_rpb"] = self.din("nat_rpb", [2, 8, 15, 31])
        d["w_out"] = self.din("w_out", [2, D, D])
        for k, s in CONST_SHAPES.items():
            d[k] = self.din("c_" + k, s)
        d["yp"] = self.dout("yp", [512, D])
        d["ys"] = self.dout("ys", [256, D])
        d["nk"] = self.dout("nk", [2, 2, 256, 512])
        d["nv"] = self.dout("nv", [2, 2, 256, 512])
        d["ns"] = self.dout("ns", [2, 2, 2, 256, 128])
        d["rp"] = nc.dram_tensor("rp_scr", [2, 8 * 17, 128], F32, kind="Internal").ap()
        self.d = d

        with ExitStack() as es:
            tk = self.tk = TK(nc, es)
            self.banks = [tk.ps(es, f"pb{i}", [128, 512]) for i in range(8)]
            self.bfree = deque(range(8))
            self.slots = [tk.sb(es, f"wslot{i}", [128, SLOT], BF16) for i in range(NW)]
            for sl_ in self.slots:
                sl_.bufs = [sl_.bufs[0], Buf("wslot_b")]
            self.plan_weights()
            self.consts(es)
            if not self.ck(0):
                self.modulation(es)
            if not self.ck(1):
                for g in range(2):
                    if not self.stopped:
                        self.run_group(g)
            assert self.stopped or self.ci == len(self.plan), (self.ci, len(self.plan))
            tk.barrier()
            self.stats = (tk.nins, tk.nwait, dict(tk.cnt))
        return nc

    def consts(self, es):
        tk, d = self.tk, self.d
        c = self.c = {}

        def ld(name, shape, src, dt=F32, q="sp"):
            v = tk.sb(es, "k_" + name, shape, dt)
            tk.dma(q, v.ap, src, writes=[v])
            c[name] = v
            return v

        ld("identf", [128, 128], d["identf"])
        ld("sel", [128, 4], d["sel"])
        for i in range(2):
            ld(f"lmat{i}", [128, 128], d["lmat"][i])
            ld(f"rmat{i}", [128, 128], d["rmat"][i])
            ld(f"mask{i}", [128, 128], d["mask"][i])
        ld("ropem", [128, 128], d["ropem"])
        ld("j64", [64, 64], d["j64"], BF16, "pool")
        ones = c["ones_d"] = tk.sb(es, "k_ones_d", [128, 128], BF16)
        tk.memset("dve", ones, 1.0 / 1024.0)
        ones = c["ones_v"] = tk.sb(es, "k_ones_v", [128, 128], BF16)
        tk.memset("dve", ones, 1.0 / 128.0)
        ones = c["ones1"] = tk.sb(es, "k_ones1", [128, 128], BF16)
        tk.memset("dve", ones, 1.0)
        self.sq = tk.sb(es, "sq", [128, 8, 512], BF16)
        self.tmp = [tk.sb(es, f"tmp{i}", [128, 512]) for i in range(3)]
        self.tmpi = 0
        self.rs = [tk.sb(es, f"rs{i}", [128, 512]) for i in range(2)]
        self.rsi = 0
        rpv = V(d["rp"], Buf("rp"))
        with ExitStack() as ls:
            for l in range(2):
                rt = tk.sb(ls, f"rpt{l}", [17, 8, 128])
                tk.memset("dve", rt, 0.0)
                tk.dma("sp", rt.ap[1:16, :, 48:79], d["nat_rpb"][l].rearrange("h a m -> a h m"), writes=[rt])
                tk.dma("sp", d["rp"][l].rearrange("(h a) m -> a h m", a=17), rt.ap, reads=[rt], writes=[rpv])
            tk.barrier()
        self.rpv = rpv

    def ntmp(self):
        self.tmpi = (self.tmpi + 1) % 3
        return self.tmp[self.tmpi]

    def nrs(self):
        self.rsi = (self.rsi + 1) % 2
        return self.rs[self.rsi]

    def modulation(self, es):
        tk, d, c = self.tk, self.d, self.c
        self.gn = tk.sb(es, "gn", [128, 2])
        self.coef = {(l, g): tk.sb(es, f"coef{l}{g}", [128, 9, 8]) for l in range(2) for g in range(2)}
        with ExitStack() as ls:
            st = tk.sb(ls, "mst", [96, 128])
            bm = [tk.sb(ls, f"mbm{l}", [72, 128]) for l in range(2)]
            cvs = tk.sb(ls, "mcv", [16, 128])
            gns = tk.sb(ls, "mgn", [2, 128])
            tk.dma("sp", st.ap, d["norm_g"], writes=[st])
            for l in range(2):
                tk.dma("sp", bm[l].ap, d["b_mod"][l], writes=[bm[l]])
            tk.dma("sp", cvs.ap, d["cvec"], writes=[cvs])
            tk.dma("sp", gns.ap, d["gla_norm_g"], writes=[gns])
            ngT = tk.sb(ls, "ngT", [128, 2, 6, 8])
            bmT = [tk.sb(ls, f"bmT{l}", [128, 72]) for l in range(2)]
            cT = tk.sb(ls, "cT", [128, 2, 8])
            p = self.bank()
            tk.transpose(p[:, 0:96], st, c["identf"][0:96, 0:96])
            tk.copy("dve", ngT.re("p l i k -> p (l i k)"), p[:, 0:96])
            for l in range(2):
                p = self.bank()
                tk.transpose(p[:, 0:72], bm[l], c["identf"][0:72, 0:72])
                tk.copy("dve", bmT[l], p[:, 0:72])
            p = self.bank()
            tk.transpose(p[:, 0:16], cvs, c["identf"][0:16, 0:16])
            tk.copy("dve", cT.re("p g k -> p (g k)"), p[:, 0:16])
            p = self.bank()
            tk.transpose(p[:, 0:2], gns, c["identf"][0:2, 0:2])
            tk.copy("dve", self.gn, p[:, 0:2])
            sc = tk.sb(ls, "sc", [128, 8, 2], BF16)
            for g in range(2):
                tk.act(sc[:, :, g], cT[:, g, :], AF.Silu)
            if self.ck(0.1):
                tk.barrier()
                return
            modT = [tk.sb(ls, f"modT{l}", [128, 72, 2]) for l in range(2)]
            for l in range(2):
                pm = self.hold()
                for b in range(18):
                    s3 = self.wnext(("mod", l, b))
                    for jj in range(4):
                        j = b * 4 + jj
                        for k in range(8):
                            tk.mm(pm[:, 2 * j:2 * j + 2], s3[:, k, jj * 128:(jj + 1) * 128], sc[:, k, :],
                                  start=(k == 0), stop=(k == 7))
                if self.ck(0.2 + l * 0.3):
                    tk.barrier()
                    return
                pm3 = pm[:, 0:144].re("p (j g) -> p j g", g=2)
                for g in range(2):
                    tk.tt("dve", modT[l][:, :, g], pm3[:, :, g], bmT[l], ALU.add)
                self.release(pm)
                for g in range(2):
                    cf = self.coef[(l, g)]
                    for s_ in range(3):
                        sh = modT[l][:, (3 * s_) * 8:(3 * s_ + 1) * 8, g]
                        scl = modT[l][:, (3 * s_ + 1) * 8:(3 * s_ + 2) * 8, g]
                        gt = modT[l][:, (3 * s_ + 2) * 8:(3 * s_ + 3) * 8, g]
                        tk.stt("dve", cf[:, 3 * s_ + 0, :], scl, 1.0, ngT[:, l, 2 * s_, :], ALU.add, ALU.mult)
                        tk.copy("dve", cf[:, 3 * s_ + 1, :], sh)
                        tk.stt("dve", cf[:, 3 * s_ + 2, :], gt, (1.0 if s_ == 1 else 0.5), ngT[:, l, 2 * s_ + 1, :],
                               ALU.mult, ALU.mult)
            tk.barrier()

    def rstd_of(self, src3, ones, W=512):
        tk = self.tk
        sq = self.sq[:, :, 0:W]
        tk.act(sq, src3, AF.Square)
        p = self.bank()
        for k in range(8):
            tk.mm(p[:, 0:W], ones, sq[:, k, :], start=(k == 0), stop=(k == 7))
        sd = self.ntmp()[:, 0:W]
        tk.act(sd, p[:, 0:W], AF.Sqrt, bias=EPS)
        r = self.nrs()[:, 0:W]
        tk.recip(r, sd)
        return r

    def prenorm(self, x, h, cf, s_, W=512):
        tk = self.tk
        r = self.rstd_of(x, self.c["ones_d"], W)
        for k in range(8):
            t = self.ntmp()[:, 0:W]
            tk.stt("dve", t, x[:, k, :], cf[:, 3 * s_, k:k + 1], r, ALU.mult, ALU.mult)
            tk.act(h[:, k, :], t, AF.Identity, bias=cf[:, 3 * s_ + 1, k:k + 1])

    def postnorm_res(self, y, x, cf, s_, W=512):
        tk = self.tk
        r = self.rstd_of(y, self.c["ones_d"], W)
        for k in range(8):
            t = self.ntmp()[:, 0:W]
            tk.stt("dve", t, y[:, k, :], cf[:, 3 * s_ + 2, k:k + 1], r, ALU.mult, ALU.mult)
            tk.tt("dve", x[:, k, :], x[:, k, :], t, ALU.add)

    def stats_a(self, src3, W, slot):
        tk = self.tk
        sq = self.sq[:, :, 0:W]
        tk.act(sq, src3, AF.Square)
        p = self.bank()
        for k in range(8):
            tk.mm(p[:, 0:W], self.c["ones_d"], sq[:, k, :], start=(k == 0), stop=(k == 7))
        sd = self.rs[slot][:, 0:W]
        tk.act(sd, p[:, 0:W], AF.Sqrt, bias=EPS)
        return sd

    def boundary(self, ys, tiles, cf, s_, nxt):
        tk = self.tk
        fused = {}
        if ys is not None:
            sds = [self.stats_a(ys[t], tiles[t][2], t) for t in range(len(tiles))]
            for sd in sds:
                tk.recip(sd, sd)
            fuse = nxt is not None and len(nxt[2]) == len(tiles) and all(
                nxt[2][t][0] is tiles[t][0] for t in range(len(tiles)))
            for t, (x, h, W) in enumerate(tiles):
                p = self.bank() if fuse else None
                for k in range(8):
                    tm = self.ntmp()[:, 0:W]
                    if k % 2 == 1:
                        tk.tt("pool", tm, ys[t][:, k, :], sds[t], ALU.mult)
                        tk.stt("dve", x[:, k, :], tm, cf[:, 3 * s_ + 2, k:k + 1], x[:, k, :], ALU.mult, ALU.add)
                    else:
                        tk.stt("dve", tm, ys[t][:, k, :], cf[:, 3 * s_ + 2, k:k + 1], sds[t], ALU.mult, ALU.mult)
                        tk.tt("dve", x[:, k, :], x[:, k, :], tm, ALU.add)
                    if fuse:
                        tk.act(self.sq[:, k, 0:W], x[:, k, :], AF.Square)
                        tk.mm(p[:, 0:W], self.c["ones_d"], self.sq[:, k, 0:W], start=(k == 0), stop=(k == 7),
                              inc=True)
                if fuse:
                    fused[t] = p
        if nxt is not None:
            ncf, ns_, ntiles = nxt
            for t, (x, h, W) in enumerate(ntiles):
                if t in fused:
                    sdx_t = self.rs[t][:, 0:W]
                    tk.act(sdx_t, fused[t][:, 0:W], AF.Sqrt, bias=EPS)
                else:
                    sdx_t = self.stats_a(x, W, t)
                tk.recip(sdx_t, sdx_t)
                for k in range(8):
                    tm = self.ntmp()[:, 0:W]
                    tk.stt("dve", tm, x[:, k, :], ncf[:, 3 * ns_, k:k + 1], sdx_t, ALU.mult, ALU.mult)
                    tk.act(h[:, k, :], tm, AF.Identity, bias=ncf[:, 3 * ns_ + 1, k:k + 1])

    def run_group(self, g):
        tk, d, c = self.tk, self.d, self.c
        ntl = 1 if g == 0 else 2
        self.g, self.ntl = g, ntl
        src = d["xp"] if g == 0 else d["xs"]
        dst = d["yp"] if g == 0 else d["ys"]
        with ExitStack() as gs:
            self.xT = [tk.sb(gs, f"xT{t}", [128, 8, 512]) for t in range(ntl)]
            self.hT = [tk.sb(gs, f"hT{t}", [128, 8, 512], BF16) for t in range(ntl)]
            if g == 1:
                self.xo = V(self.hT[1].ap.bitcast(F32), self.hT[1].bufs)
            ss = ExitStack()
            stage = tk.sb(ss, "stage", [128, 4, 1024])
            for t in range(ntl):
                tk.dma("sp", stage.ap, src[t * 512:(t + 1) * 512, :].rearrange("(i p) f -> p i f", p=128),
                       writes=[stage])
                for k in range(8):
                    p = self.bank()
                    for i in range(4):
                        tk.transpose(p[:, i * 128:(i + 1) * 128], stage[:, i, k * 128:(k + 1) * 128], c["identf"])
                    tk.copy("act" if k % 2 else "dve", self.xT[t][:, k, :], p)
            tk.barrier()
            ss.close()
            self.ck(100 * g + 2)
            subs = [(l, kind) for l in range(2) for kind in (0, 1, 2)]
            own_tiles = [(self.xo, self.hT[0][:, :, 0:256], 256)] if g == 1 else None
            for si_, (l, kind) in enumerate(subs):
                if self.stopped:
                    break
                cf = self.coef[(l, g)]
                full_tiles = [(self.xT[t], self.hT[t], 512) for t in range(ntl)]
                nxt = None
                if si_ + 1 < len(subs):
                    nl, nkind = subs[si_ + 1]
                    ntiles = own_tiles if (g == 1 and (nl, nkind) == (1, 2)) else full_tiles
                    nxt = (self.coef[(nl, g)], nkind, ntiles)
                pre = (si_ == 0)
                if kind == 1:
                    self.mixer(l, cf, do_pre=pre, nxt=nxt)
                elif g == 1 and (l, kind) == (1, 2):
                    self.ffn(l, 1, cf, 2, tiles=own_tiles, do_pre=pre, nxt=nxt)
                else:
                    self.ffn(l, 0 if kind == 0 else 1, cf, kind, do_pre=pre, nxt=nxt)
                self.ck(100 * g + 10 * (kind + 1) + 30 * l)
            stage = tk.sb(gs, "stage", [128, 4, 1024])
            if g == 0:
                for t in range(ntl):
                    for i in range(4):
                        for hf in range(2):
                            p = self.bank()
                            for kk in range(4):
                                tk.transpose(p[:, kk * 128:(kk + 1) * 128],
                                             self.xT[t][:, hf * 4 + kk, i * 128:(i + 1) * 128], c["identf"])
                            tk.copy("act" if hf else "dve", stage[:, i, hf * 512:(hf + 1) * 512], p)
                    tk.dma("sp", dst[t * 512:(t + 1) * 512, :].rearrange("(i p) f -> p i f", p=128), stage.ap,
                           reads=[stage])
            else:
                for i in range(2):
                    for hf in range(2):
                        p = self.bank()
                        for kk in range(4):
                            tk.transpose(p[:, kk * 128:(kk + 1) * 128],
                                         self.xo[:, hf * 4 + kk, i * 128:(i + 1) * 128], c["identf"])
                        tk.copy("act" if hf else "dve", stage[:, i, hf * 512:(hf + 1) * 512], p)
                tk.dma("sp", dst.rearrange("(i p) f -> p i f", p=128), stage.ap[:, 0:2, :], reads=[stage])
            tk.barrier()

    def ffn(self, l, f, cf, s_, tiles=None, do_pre=True, nxt=None):
        tk = self.tk
        g = self.g
        if tiles is None:
            tiles = [(self.xT[t], self.hT[t], 512) for t in range(self.ntl)]
        ntl = len(tiles)
        with ExitStack() as fs:
            big = [tk.sb(fs, f"big{t}", [128, FC, tiles[t][2]], BF16) for t in range(ntl)]
            yT = [tk.sb(fs, f"yT{t}", [128, 8, tiles[t][2]]) for t in range(ntl)]
            if do_pre:
                self.boundary(None, None, None, None, (cf, s_, tiles))
            for b in range(11):
                s3 = self.wnext(("fin", g, l, f, b))
                for jj in range(2):
                    j = 2 * b + jj
                    for t in range(ntl):
                        x_, h_, W = tiles[t]
                        pa = self.bank()
                        pb = self.bank()
                        for k in range(8):
                            tk.mm(pa[:, 0:W], s3[:, k, jj * 128:(jj + 1) * 128], h_[:, k, :],
                                  start=(k == 0), stop=(k == 7))
                        for k in range(8):
                            tk.mm(pb[:, 0:W], s3[:, k, 256 + jj * 128:256 + (jj + 1) * 128], h_[:, k, :],
                                  start=(k == 0), stop=(k == 7))
                        tm = self.ntmp()[:, 0:W]
                        tk.act(tm, pa[:, 0:W], AF.Silu)
                        tk.tt("dve", big[t][:, j, :], tm, pb[:, 0:W], ALU.mult)
            ev = 0
            for mp in range(4):
                pbs = [[self.hold() for t in range(ntl)] for mm_ in range(2)]
                for kh in range(2):
                    s3 = self.wnext(("fout", g, l, f, mp, kh))
                    for mm_ in range(2):
                        for t in range(ntl):
                            W = tiles[t][2]
                            p = pbs[mm_][t]
                            for k in range(11):
                                tk.mm(p[:, 0:W], s3[:, k, mm_ * 128:(mm_ + 1) * 128], big[t][:, kh * 11 + k, :],
                                      start=(kh == 0 and k == 0), stop=(kh == 1 and k == 10), inc=(k == 10))
                for mm_ in range(2):
                    for t in range(ntl):
                        W = tiles[t][2]
                        tk.copy("act" if ev % 2 else "dve", yT[t][:, 2 * mp + mm_, :], pbs[mm_][t][:, 0:W])
                        ev += 1
                        self.release(pbs[mm_][t])
            self.boundary(yT, tiles, cf, s_, nxt)
            tk.barrier()

    def mixer(self, l, cf, do_pre=True, nxt=None):
        tk, d, c = self.tk, self.d, self.c
        g, ntl = self.g, self.ntl
        T = 512 * ntl
        nt = 4 * ntl
        lat = (g == 1)
        seqs = [list(range(nt))] if lat else [[0, 1], [2, 3]]
        hT = self.hT

        def hcols(i):
            t, ii = divmod(i, 4)
            return [hT[t][:, k, ii * 128:(ii + 1) * 128] for k in range(8)]

        with ExitStack() as ms:
            if do_pre:
                self.boundary(None, None, None, None, (cf, 1, [(self.xT[t], hT[t], 512) for t in range(ntl)]))
            mixT = tk.sb(ms, "mixT", [128, 8, T], BF16)
            own = lat and l == 1
            with ExitStack() as gs:
                sr = tk.sb(gs, "sr", [128, 4, T], BF16)
                vtok = tk.sb(gs, "vtok", [128, nt, 512], BF16)
                qs = [[tk.sb(gs, f"qs{i}{hp}", [128, T], BF16) for hp in range(2)] for i in range(2)]
                ks = [[tk.sb(gs, f"ks{i}{hp}", [128, T], BF16) for hp in range(2)] for i in range(2)]
                kd = [tk.sb(gs, f"kd{i}", [128, nt, 256], BF16) for i in range(2)]
                dec = [[tk.sb(gs, f"dec{i}{hp}", [128, nt]) for hp in range(2)] for i in range(2)]
                pa = ExitStack()
                qT = tk.sb(pa, "qT", [128, 2, T])
                kT = tk.sb(pa, "kT", [128, 2, T])
                lrA = [tk.sb(pa, f"lrA{i}", [32, T], BF16) for i in range(2)]
                ktok = tk.sb(pa, "ktok", [128, nt, 256])
                wa2a = [tk.sb(pa, f"wa2a{i}", [17, 256], BF16) for i in range(2)]
                gpb = [tk.sb(pa, f"gp{i}", [128, 256]) for i in range(2)]
                e1b = [tk.sb(pa, f"e1{i}", [128, 256]) for i in range(2)]
                ETb = [tk.sb(pa, f"ET{i}", [128, 128]) for i in range(4)]
                EIb = [tk.sb(pa, f"EI{i}", [128, 128]) for i in range(4)]
                ERb = [tk.sb(pa, f"ER{i}", [128, 256]) for i in range(2)]
                if lat:
                    rcos = tk.sb(pa, "rcos", [128, 1024])
                    rsin = tk.sb(pa, "rsin", [128, 1024])
                    tk.dma("sp", rcos.ap, d["ropecos"], writes=[rcos])
                    tk.dma("sp", rsin.ap, d["ropesin"], writes=[rsin])
                for i in range(2):
                    tk.dma("pool", wa2a[i].ap[0:16, :], d["gla_wa2"][l, i], writes=[wa2a[i]])
                    tk.dma("pool", wa2a[i].ap[16:17, :], d["gla_ba"][l, i], writes=[wa2a[i]])
                    tk.memset("dve", lrA[i], 1.0)
                s3 = self.wnext(("win", g, l, 0))
                for cch in range(4):
                    for t in range(ntl):
                        p = self.bank()
                        for k in range(8):
                            tk.mm(p, s3[:, k, cch * 128:(cch + 1) * 128], hT[t][:, k, :], start=(k == 0), stop=(k == 7))
                        dstv = (qT if cch < 2 else kT)[:, cch % 2, t * 512:(t + 1) * 512]
                        tk.copy("act" if (cch + t) % 2 else "dve", dstv, p)
                for i in range(nt if not lat else 0):
                    p = self.bank()
                    hc = hcols(i)
                    for k in range(8):
                        tk.mm(p[:, 0:256], hc[k], s3[:, k, 256:512], start=(k == 0), stop=(k == 7))
                    tk.copy("act" if i % 2 else "dve", ktok[:, i, :], p[:, 0:256])
                s3 = self.wnext(("win", g, l, 1))
                for i in range(nt):
                    p = self.bank()
                    hc = hcols(i)
                    for k in range(8):
                        tk.mm(p, hc[k], s3[:, k, :], start=(k == 0), stop=(k == 7))
                    tk.copy("act" if i % 2 else "dve", vtok[:, i, :], p)
                s3 = self.wnext(("win", g, l, 2))
                for cch in range(4):
                    for t in range(ntl):
                        p = self.bank()
                        for k in range(8):
                            tk.mm(p, s3[:, k, cch * 128:(cch + 1) * 128], hT[t][:, k, :], start=(k == 0), stop=(k == 7))
                        tk.act(sr[:, cch, t * 512:(t + 1) * 512], p, AF.Silu)
                s3 = self.wnext(("win", g, l, 3))
                for i in range(2):
                    for t in range(ntl):
                        p = self.bank()
                        for k in range(8):
                            tk.mm(p[0:16, :], s3[:, k, i * 16:(i + 1) * 16], hT[t][:, k, :], start=(k == 0), stop=(k == 7))
                        tk.copy("dve", lrA[i][0:16, t * 512:(t + 1) * 512], p[0:16, :])
                if lat:
                    for xx in (qT, kT):
                        xx.bufs = [xx.bufs[0]] + [Buf("rope") for _ in range(2 * ntl - 1)]
                    tk.barrier(("pe", "dve", "act"))
                    for xx in (qT, kT):
                        for hp in range(2):
                            for t in range(ntl):
                                cs = slice(t * 512, (t + 1) * 512)
                                xv = V(xx.ap[:, hp, cs], [xx.bufs[hp * ntl + t]])
                                p = self.bank()
                                tk.mm(p, c["ropem"], xv)
                                t1 = self.ntmp()
                                tk.tt("dve", t1, p, rsin[:, cs], ALU.mult)
                                tk.tt("dve", xv, xv, rcos[:, cs], ALU.mult)
                                tk.tt("dve", xv, xv, t1, ALU.add)
                if lat:
                    for i in range(nt):
                        p = self.bank()
                        for hp in range(2):
                            tk.transpose(p[:, hp * 128:(hp + 1) * 128], kT[:, hp, i * 128:(i + 1) * 128], c["identf"])
                        tk.copy("act" if i % 2 else "dve", ktok[:, i, :], p[:, 0:256])
                gitems = [(i, dr) for i in range(nt) for dr in range(2)]

                def G1(n):
                    i, dr = gitems[n]
                    cs = slice(i * 128, (i + 1) * 128)
                    pz = self.bank()
                    tk.mm(pz[:, 0:256], lrA[dr][0:17, cs], wa2a[dr][0:17, :])
                    tk.act(e1b[dr], pz[:, 0:256], AF.Exp, scale=-1.0)
                    tk.act(gpb[dr], e1b[dr], AF.Ln, bias=1.0)

                def G2(n):
                    i, dr = gitems[n]
                    cs = slice(i * 128, (i + 1) * 128)
                    gp = gpb[dr]
                    pcs = []
                    for hp in range(2):
                        pc = self.bank()
                        tk.mm(pc[:, 0:128], gp[:, hp * 128:(hp + 1) * 128], c[f"lmat{dr}"])
                        pcs.append(pc)
                    pr = self.bank()
                    tk.mm(pr[:, 0:256], c[f"rmat{dr}"], gp)
                    for hp in range(2):
                        ET, EI = ETb[2 * dr + hp], EIb[2 * dr + hp]
                        tk.act(ET, pcs[hp][:, 0:128], AF.Exp)
                        tk.act(EI, pcs[hp][:, 0:128], AF.Exp, scale=-1.0)
                        tk.stt("dve", qs[dr][hp][:, cs], qT[:, hp, cs], 0.125, ET, ALU.mult, ALU.mult)
                        tk.tt("dve", ks[dr][hp][:, cs], kT[:, hp, cs], EI, ALU.mult)
                        lc = 127 if dr == 0 else 0
                        tk.copy("dve", dec[dr][hp][:, i:i + 1], ET[:, lc:lc + 1])
                    ER = ERb[dr]
                    tk.act(ER, pr[:, 0:256], AF.Exp)
                    tk.tt("dve", kd[dr][:, i, :], ktok[:, i, :], ER, ALU.mult)

                G1(0)
                for n in range(len(gitems)):
                    if n + 1 < len(gitems):
                        G1(n + 1)
                    G2(n)
                tk.barrier()
                pa.close()
                S = [[tk.sb(gs, f"S{i}{hp}", [128, 128]) for hp in range(2)] for i in range(2)]
                Sb = [[[tk.sb(gs, f"Sb{i}{hp}{j}", [128, 128], BF16) for j in range(nt)] for hp in range(2)]
                      for i in range(2)]
                smb = [tk.sb(gs, f"sm{i}", [128, 128], BF16) for i in range(8)]
                osqb = [tk.sb(gs, f"osq{i}", [128, 512], BF16) for i in range(2)]
                for si, tiles in enumerate(seqs):
                    chains = [(dr, hp) for dr in range(2) for hp in range(2)]
                    for dr, hp in chains:
                        St = S[dr][hp]
                        if lat:
                            tk.dma("sp", St.ap, d["sg"][l, dr, hp * 128:(hp + 1) * 128, :], writes=[St])
                        else:
                            tk.memset("dve", St, 0.0)
                    for step in range(len(tiles)):
                        for dr, hp in chains:
                            St = S[dr][hp]
                            i = tiles[step] if dr == 0 else tiles[-1 - step]
                            tk.copy("act", Sb[dr][hp][i], St)
                            pk = self.bank()
                            for e in range(2):
                                h = 2 * hp + e
                                tk.mm(pk[e * 64:(e + 1) * 64, 0:128], kd[dr][:, i, h * 64:(h + 1) * 64],
                                      vtok[:, i, h * 128:(h + 1) * 128])
                            tk.stt("dve", St, St, dec[dr][hp][:, i:i + 1], pk[:, 0:128], ALU.mult, ALU.add)
                    if not lat:
                        for dr, hp in chains:
                            tk.dma("sp", d["ns"][si, l, dr, hp * 128:(hp + 1) * 128, :], S[dr][hp].ap,
                                   reads=[S[dr][hp]])
                oitems = [(blk, h, ii) for blk in range(ntl) for h in range(4) for ii in range(4)]
                OLA = 2
                smd, pos = {}, {}
                smi = [0]

                def P1(it):
                    blk, h, ii = it
                    hp, e = divmod(h, 2)
                    es_ = slice(e * 64, (e + 1) * 64)
                    i = blk * 4 + ii
                    cs = slice(i * 128, (i + 1) * 128)
                    sms = []
                    for dr in range(2):
                        psc = self.bank()
                        tk.mm(psc[:, 0:128], ks[dr][hp][es_, cs], qs[dr][hp][es_, cs])
                        sm = smb[smi[0] % len(smb)]
                        smi[0] += 1
                        tk.tt("dve", sm, psc[:, 0:128], c[f"mask{dr}"], ALU.mult)
                        sms.append(sm)
                    smd[it] = sms

                def P2(it):
                    blk, h, ii = it
                    hp, e = divmod(h, 2)
                    es_ = slice(e * 64, (e + 1) * 64)
                    i = blk * 4 + ii
                    cs = slice(i * 128, (i + 1) * 128)
                    if ii == 0:
                        pos[(blk, h)] = self.hold()
                    po = pos[(blk, h)]
                    sms = smd.pop(it)
                    oc = po[:, ii * 128:(ii + 1) * 128]
                    tk.mm(oc, vtok[:, i, h * 128:(h + 1) * 128], sms[0], start=True, stop=False)
                    tk.mm(oc, Sb[0][hp][i][es_, :], qs[0][hp][es_, cs], start=False, stop=False)
                    tk.mm(oc, vtok[:, i, h * 128:(h + 1) * 128], sms[1], start=False, stop=False)
                    tk.mm(oc, Sb[1][hp][i][es_, :], qs[1][hp][es_, cs], start=False, stop=True)
                    if ii == 3:
                        osq = osqb[(blk * 4 + h) % 2]
                        tk.act(osq, po, AF.Square)
                        p2 = self.bank()
                        tk.mm(p2, c["ones_v"], osq)
                        sd = self.ntmp()
                        tk.act(sd, p2, AF.Sqrt, bias=EPS)
                        r = self.nrs()
                        tk.recip(r, sd)
                        t1 = self.ntmp()
                        tk.stt("dve", t1, po, self.gn[:, l:l + 1], r, ALU.mult, ALU.mult)
                        tk.tt("dve", mixT[:, h, blk * 512:(blk + 1) * 512], t1, sr[:, h, blk * 512:(blk + 1) * 512],
                              ALU.mult)
                        self.release(po)

                for n in range(len(oitems) + OLA):
                    if n < len(oitems):
                        P1(oitems[n])
                    if n >= OLA:
                        P2(oitems[n - OLA])
                tk.barrier()
            if self.ck(100 * g + 15 + 30 * l):
                return
            mixo = tk.sb(ms, "mixo", [128, 8, 256], BF16) if own else None
            with ExitStack() as ns:
                qn = tk.sb(ns, "qn", [128, 4, T], BF16)
                kn = tk.sb(ns, "kn", [128, 4, T], BF16)
                vn = tk.sb(ns, "vn", [128, nt, 512], BF16)
                exb = [tk.sb(ns, f"ex{i}", [128, 512], BF16) for i in range(6)]
                exi = 0
                rden = tk.sb(ns, "rden", [128, 512])
                if not lat:
                    self.stage = tk.sb(ns, "stage", [128, 4, 1024])
                s3 = self.wnext(("win", g, l, 4))
                for cch in range(4):
                    for t in range(ntl):
                        p = self.bank()
                        for k in range(8):
                            tk.mm(p, s3[:, k, cch * 128:(cch + 1) * 128], hT[t][:, k, :], start=(k == 0), stop=(k == 7))
                        tk.act(qn[:, cch, t * 512:(t + 1) * 512], p, AF.Copy, scale=0.125)
                s3 = self.wnext(("win", g, l, 5))
                for cch in range(4):
                    for t in range(ntl):
                        p = self.bank()
                        for k in range(8):
                            tk.mm(p, s3[:, k, cch * 128:(cch + 1) * 128], hT[t][:, k, :], start=(k == 0), stop=(k == 7))
                        tk.copy("act" if (cch + t) % 2 else "dve", kn[:, cch, t * 512:(t + 1) * 512], p)
                if not lat:
                    for i in range(nt):
                        p = self.bank()
                        hc = hcols(i)
                        for k in range(8):
                            tk.mm(p, hc[k], s3[:, k, :], start=(k == 0), stop=(k == 7))
                        tk.copy("dve", self.stage[:, i, 0:512], p)
                    for si in range(2):
                        tk.dma("sp", d["nk"][si, l].rearrange("(i p) f -> p i f", p=128),
                               self.stage.ap[:, 2 * si:2 * si + 2, 0:512], reads=[self.stage])
                s3 = self.wnext(("win", g, l, 6))
                for i in range(nt):
                    p = self.bank()
                    hc = hcols(i)
                    for k in range(8):
                        tk.mm(p, hc[k], s3[:, k, :], start=(k == 0), stop=(k == 7))
                    tk.copy("act", vn[:, i, :], p)
                    if not lat:
                        tk.copy("dve", self.stage[:, i, 512:1024], p)
                if not lat:
                    for si in range(2):
                        tk.dma("sp", d["nv"][si, l].rearrange("(i p) f -> p i f", p=128),
                               self.stage.ap[:, 2 * si:2 * si + 2, 512:1024], reads=[self.stage])
                    citems = [(hp, e, si) for hp in range(4) for e in range(2) for si in range(2)]
                    CLA = 2
                    cex, cpods = {}, {}
                    exn = [0]

                    def cA(it):
                        hp, e, si = it
                        es_ = slice(e * 64, (e + 1) * 64)
                        qsl = slice(si * 256, (si + 1) * 256)
                        psc = self.bank()
                        for kt in range(2):
                            i = 2 * si + kt
                            tk.mm(psc[:, kt * 256:(kt + 1) * 256], kn[es_, hp, i * 128:(i + 1) * 128], qn[es_, hp, qsl])
                        ex = exb[exn[0] % len(exb)]
                        exn[0] += 1
                        tk.act(ex, psc, AF.Exp)
                        cex[it] = ex

                    def cC(it):
                        hp, e, si = it
                        h = 2 * hp + e
                        es_ = slice(e * 64, (e + 1) * 64)
                        qsl = slice(si * 256, (si + 1) * 256)
                        if e == 0 and si == 0:
                            cpods[hp] = (self.hold(), self.hold())
                        po, pd = cpods[hp]
                        ex = cex.pop(it)
                        for kt in range(2):
                            i = 2 * si + kt
                            tk.mm(po[es_, qsl], vn[:, i, h * 64:(h + 1) * 64], ex[:, kt * 256:(kt + 1) * 256],
                                  start=(kt == 0), stop=(kt == 1), inc=True)
                        for kt in range(2):
                            tk.mm(pd[es_, qsl], c["ones1"][:, 0:64], ex[:, kt * 256:(kt + 1) * 256],
                                  start=(kt == 0), stop=(kt == 1), inc=True)
                        if e == 1 and si == 1:
                            tk.recip(rden, pd)
                            tk.tt("dve", mixT[:, 4 + hp, :], po, rden, ALU.mult)
                            self.release(po)
                            self.release(pd)

                    for n in range(len(citems) + CLA):
                        if n < len(citems):
                            cA(citems[n])
                        if n >= CLA:
                            cC(citems[n - CLA])
                else:
                    ckT = tk.sb(ns, "ckT", [128, 4, 512], BF16)
                    cvt = tk.sb(ns, "cvt", [128, 4, 512], BF16)
                    c["ohk"] = tk.sb(ns, "ohk", [128, 1024], BF16)
                    c["pen"] = tk.sb(ns, "pen", [128, 1024], BF16)
                    tk.dma("pool", c["ohk"].ap, d["ohk"], writes=[c["ohk"]])
                    tk.dma("pool", c["pen"].ap, d["pen"], writes=[c["pen"]])
                    Bt = tk.sb(ns, "Bt", [128, 8, 16, 64], BF16)
                    tk.dma("pool", cvt.ap, d["cv"][l].rearrange("(a p) f -> p a f", p=128), writes=[cvt])
                    with ExitStack() as cks:
                        ckf = tk.sb(cks, "ckf", [128, 4, 512])
                        tk.dma("sp", ckf.ap, d["ck"][l].rearrange("(a p) f -> p a f", p=128), writes=[ckf])
                        for a in range(4):
                            p = self.bank()
                            for hp in range(4):
                                tk.transpose(p[:, hp * 128:(hp + 1) * 128], ckf[:, a, hp * 128:(hp + 1) * 128],
                                             c["identf"])
                            tk.copy("dve", ckT.re("p h (a k) -> p h a k", a=4)[:, :, a, :],
                                    p.re("p (h k) -> p h k", h=4))
                        tk.barrier()
                    with ExitStack() as tts:
                        TT = tk.sb(tts, "TT", [64, 8 * 17 * 64], BF16)
                        rp = d["rp"]
                        TTh = TT.ap.rearrange("p (h a c) -> p h a c", h=8, a=17)
                        ttb = [Buf(f"tt{h8}") for h8 in range(8)]
                        for h8 in range(8):
                            srcap = bass.AP(tensor=rp.tensor, offset=(l * 8 + h8) * 17 * 128, ap=[[1, 64], [128, 17], [1, 64]])
                            tk.dma("pool", TTh[:, h8], srcap, reads=[self.rpv], writes=[V(TT.ap, [ttb[h8]])])
                        TT4 = TT.re("p (h a c) -> p h a c", h=8, a=17)
                        for h8 in range(8):
                            for grp in range(2):
                                p = self.bank()
                                for ii in range(8):
                                    idx = grp * 8 + ii
                                    lh = V(TT4.ap[:, h8, 15 - idx:17 - idx, :].rearrange("p a c -> p (a c)"), [ttb[h8]])
                                    tk.mm(p[:, ii * 64:(ii + 1) * 64], lh, c["j64"])
                                tk.copy("act" if grp else "dve",
                                        Bt[:, h8, grp * 8:(grp + 1) * 8, :].re("p a c -> p (a c)"), p)
                        tk.barrier()
                    ohk3 = c["ohk"].re("p (t k) -> p t k", t=8)
                    if self.ck(100 * g + 17 + 30 * l):
                        tk.barrier()
                        return
                    if own:
                        sel = c["sel"]
                        qno = tk.sb(ns, "qno", [128, 4, 256], BF16)
                        peno = tk.sb(ns, "peno", [128, 256], BF16)
                        tk.dma("pool", peno.ap, d["pen_own"], writes=[peno])
                        for r in range(4):
                            qsl_ = qn[:, :, r * 256:(r + 1) * 256]
                            if r == 0:
                                tk.ts("dve", qno, qsl_, sel[:, 0:1], None, op0=ALU.mult)
                            else:
                                tk.stt("dve", qno, qsl_, sel[:, r:r + 1], qno, ALU.mult, ALU.add)
                        pos_ = [self.hold(), self.hold()]
                        pds_ = [self.hold(), self.hold()]
                        oitems_ = [(kt, hp, e) for hp in range(4) for e in range(2) for kt in range(12)]
                        OLA_ = int(os.environ.get("KOLA", "3"))
                        opsc, oexs = {}, {}
                        ocnt = [0]

                        def oA(it):
                            kt, hp, e = it
                            h = 2 * hp + e
                            es_ = slice(e * 64, (e + 1) * 64)
                            psc = self.bank()
                            if kt < 8:
                                tk.mm(psc[:, 0:256], kn[es_, hp, kt * 128:(kt + 1) * 128], qno[es_, hp, :],
                                      start=True, stop=False)
                                tk.mm(psc[:, 0:256], ohk3[:, kt, :], peno, start=False, stop=True)
                                for r in range(4):
                                    i0_ = 7 - 2 * kt + 4 * r
                                    jlo = max(0, -i0_)
                                    jhi = min(4, 16 - i0_)
                                    if jlo < jhi:
                                        src_ = Bt[:, h, i0_ + jlo:i0_ + jhi, :].re("p a c -> p (a c)")
                                        pv_ = psc[:, jlo * 64:jhi * 64]
                                        tk.stt("dve", pv_, src_, sel[:, r:r + 1], pv_, ALU.mult, ALU.add)
                            else:
                                a = kt - 8
                                tk.mm(psc[:, 0:256], ckT[es_, hp, a * 128:(a + 1) * 128], qno[es_, hp, :])
                            opsc[it] = psc

                        def oB(it):
                            ex = exb[ocnt[0] % len(exb)]
                            ocnt[0] += 1
                            tk.act(ex[:, 0:256], opsc.pop(it)[:, 0:256], AF.Exp)
                            oexs[it] = ex

                        def oC(it):
                            kt, hp, e = it
                            h = 2 * hp + e
                            es_ = slice(e * 64, (e + 1) * 64)
                            csl = slice((hp % 2) * 256, (hp % 2 + 1) * 256)
                            po, pd = pos_[hp // 2], pds_[hp // 2]
                            vv = vn[:, kt, h * 64:(h + 1) * 64] if kt < 8 else cvt[:, kt - 8, h * 64:(h + 1) * 64]
                            ex = oexs.pop(it)
                            tk.mm(po[es_, csl], vv, ex[:, 0:256], start=(kt == 0), stop=(kt == 11), inc=True)
                            tk.mm(pd[es_, csl], c["ones1"][:, 0:64], ex[:, 0:256], start=(kt == 0), stop=(kt == 11),
                                  inc=True)

                        for n in range(len(oitems_) + OLA_):
                            if n < len(oitems_):
                                oA(oitems_[n])
                            if n >= OLA_:
                                oB(oitems_[n - OLA_])
                                oC(oitems_[n - OLA_])
                        for hp2 in range(2):
                            tk.recip(rden, pds_[hp2])
                            tk.tt("dve", mixo[:, 4 + 2 * hp2:6 + 2 * hp2, :],
                                  pos_[hp2].re("p (a c) -> p a c", a=2), rden.re("p (a c) -> p a c", a=2), ALU.mult)
                            self.release(pos_[hp2])
                            self.release(pds_[hp2])
                    else:
                        kts = {0: [0, 1, 2, 3, 4, 5, 8, 9, 10, 11], 1: [2, 3, 4, 5, 6, 7, 8, 9, 10, 11]}
                        items = [(hp, Q, e, kt) for hp in range(4) for Q in range(2) for e in range(2)
                                 for kt in kts[Q]]
                        LA = 3
                        pscs, exs, pods = {}, {}, {}

                        def stA(it):
                            hp, Q, e, kt = it
                            h = 2 * hp + e
                            es_ = slice(e * 64, (e + 1) * 64)
                            qsl = slice(Q * 512, (Q + 1) * 512)
                            psc = self.bank()
                            if kt < 8:
                                tk.mm(psc, kn[es_, hp, kt * 128:(kt + 1) * 128], qn[es_, hp, qsl], start=True, stop=False)
                                tk.mm(psc, ohk3[:, kt, :], c["pen"][:, qsl], start=False, stop=True)
                                jlo = max(0, 2 * kt - 8 * Q - 7)
                                jhi = min(8, 2 * kt - 8 * Q + 9)
                                if jlo < jhi:
                                    ilo = 7 - 2 * kt + 8 * Q + jlo
                                    bsl = Bt[:, h, ilo:ilo + (jhi - jlo), :].re("p a c -> p (a c)")
                                    tk.tt("dve", psc[:, jlo * 64:jhi * 64], psc[:, jlo * 64:jhi * 64], bsl, ALU.add)
                            else:
                                a = kt - 8
                                tk.mm(psc, ckT[es_, hp, a * 128:(a + 1) * 128], qn[es_, hp, qsl])
                            pscs[it] = psc

                        def stB(it):
                            ex = exb[len(exs) % len(exb)]
                            tk.act(ex, pscs.pop(it), AF.Exp)
                            exs[it] = ex

                        def stC(it):
                            hp, Q, e, kt = it
                            h = 2 * hp + e
                            es_ = slice(e * 64, (e + 1) * 64)
                            qsl = slice(Q * 512, (Q + 1) * 512)
                            if e == 0 and kt == kts[Q][0]:
                                pods[(hp, Q)] = (self.hold(), self.hold())
                            po, pd = pods[(hp, Q)]
                            vv = vn[:, kt, h * 64:(h + 1) * 64] if kt < 8 else cvt[:, kt - 8, h * 64:(h + 1) * 64]
                            ex = exs[it]
                            tk.mm(po[es_, :], vv, ex, start=(kt == kts[Q][0]), stop=(kt == 11), inc=True)
                            tk.mm(pd[es_, :], c["ones1"][:, 0:64], ex, start=(kt == kts[Q][0]), stop=(kt == 11), inc=True)
                            if e == 1 and kt == 11:
                                tk.recip(rden, pd)
                                tk.tt("dve", mixT[:, 4 + hp, qsl], po, rden, ALU.mult)
                                self.release(po)
                                self.release(pd)

                        for n in range(len(items) + LA):
                            if n < len(items):
                                stA(items[n])
                            if n >= LA:
                                stB(items[n - LA])
                                stC(items[n - LA])
                tk.barrier()
            with ExitStack() as os_:
                if lat and l == 1:
                    sel = c["sel"]
                    xo = self.xo
                    yTo = tk.sb(os_, "yTo", [128, 8, 256])
                    for r in range(4):
                        msl = mixT[:, :, r * 256:(r + 1) * 256]
                        xsl = self.xT[r // 2][:, :, (r % 2) * 256:(r % 2 + 1) * 256]
                        msl = mixT[:, 0:4, r * 256:(r + 1) * 256]
                        if r == 0:
                            tk.ts("dve", mixo[:, 0:4, :], msl, sel[:, 0:1], None, op0=ALU.mult)
                            tk.ts("dve", xo, xsl, sel[:, 0:1], None, op0=ALU.mult)
                        else:
                            tk.stt("dve", mixo[:, 0:4, :], msl, sel[:, r:r + 1], mixo[:, 0:4, :], ALU.mult, ALU.add)
                            tk.stt("dve", xo, xsl, sel[:, r:r + 1], xo, ALU.mult, ALU.add)
                    ev = 0
                    for b in range(2):
                        s3 = self.wnext(("wout", g, l, b))
                        for mm_ in range(4):
                            m = b * 4 + mm_
                            p = self.bank()
                            for k in range(8):
                                tk.mm(p[:, 0:256], s3[:, k, mm_ * 128:(mm_ + 1) * 128], mixo[:, k, :],
                                      start=(k == 0), stop=(k == 7))
                            tk.copy("act" if ev % 2 else "dve", yTo[:, m, :], p[:, 0:256])
                            ev += 1
                    self.boundary([yTo], [(xo, None, 256)], cf, 1, nxt)
                else:
                    yT = [tk.sb(os_, f"yTm{t}", [128, 8, 512]) for t in range(ntl)]
                    ev = 0
                    for b in range(2):
                        s3 = self.wnext(("wout", g, l, b))
                        for mm_ in range(4):
                            m = b * 4 + mm_
                            for t in range(ntl):
                                p = self.bank()
                                for k in range(8):
                                    tk.mm(p, s3[:, k, mm_ * 128:(mm_ + 1) * 128], mixT[:, k, t * 512:(t + 1) * 512],
                                          start=(k == 0), stop=(k == 7))
                                tk.copy("act" if ev % 2 else "dve", yT[t][:, m, :], p)
                                ev += 1
                    self.boundary(yT, [(self.xT[t], hT[t], 512) for t in range(ntl)], cf, 1, nxt)
                tk.barrier()


_NC_CACHE = {}


def build_nc():
    if "nc" not in _NC_CACHE:
        nc = bass.Bass("TRN2", target_bir_lowering=False)
        Kern(nc).build()
        _NC_CACHE["nc"] = nc
    return _NC_CACHE["nc"]


def kernel(x_prompt, x_sample, cache_k, cache_v, state_gla, c, c_ctx, w_mod, b_mod, norm_g,
           ffn_w_in, ffn_w_out, w_in, gla_wa2, gla_ba, gla_norm_g, nat_rpb, w_out):
    f = lambda a: np.ascontiguousarray(np.asarray(a, dtype=np.float32))
    x_prompt, x_sample, cache_k, cache_v, state_gla = map(f, (x_prompt, x_sample, cache_k, cache_v, state_gla))
    c, c_ctx = f(c), f(c_ctx)
    consts = make_consts()
    shared = {
        "w_mod": f(w_mod), "b_mod": f(b_mod).reshape(2, 72, 128), "norm_g": f(norm_g).reshape(96, 128),
        "ffn_w_in": f(ffn_w_in), "ffn_w_out": f(ffn_w_out), "w_in": f(w_in), "gla_wa2": f(gla_wa2),
        "gla_ba": f(gla_ba).reshape(2, 2, 1, 256), "gla_norm_g": f(gla_norm_g), "nat_rpb": f(nat_rpb),
        "w_out": f(w_out),
    }
    for k, v in consts.items():
        shared["c_" + k] = v
    in_maps = []
    cores = [int(x) for x in os.environ.get("KCORES", "0,1,2,3,4,5,6,7").split(",")]
    for core in cores:
        b = core // 4
        m = dict(shared)
        m["xp"] = x_prompt[2 * core:2 * core + 2].reshape(512, D)
        m["xs"] = x_sample[b]
        m["ck"] = cache_k[b].reshape(2, 512, 512)
        m["cv"] = cache_v[b].reshape(2, 512, 512)
        m["sg"] = state_gla[b].reshape(2, 2, 256, 128)
        m["cvec"] = np.ascontiguousarray(np.stack([c_ctx, c[b]])).reshape(16, 128)
        selv = np.zeros((128, 4), np.float32)
        selv[:, core % 4] = 1.0
        m["sel"] = selv
        m["pen_own"] = np.ascontiguousarray(consts["pen"][:, (core % 4) * 256:(core % 4 + 1) * 256])
        in_maps.append(m)
    nc = build_nc()
    res = run_bass_kernel_spmd(nc, in_maps, core_ids=list(range(len(cores))))
    R = res.results
    if len(cores) != 8:
        return {cores[i]: R[i] for i in range(len(cores))}
    y_prompt = np.stack([R[i]["yp"].reshape(2, 256, D) for i in range(8)]).reshape(16, 256, D)
    y_sample = np.stack([R[i]["ys"] for i in range(8)]).reshape(2, 1024, D)
    nk = np.stack([R[i]["nk"] for i in range(8)]).reshape(16, 2, 256, 8, 64)
    nv = np.stack([R[i]["nv"] for i in range(8)]).reshape(16, 2, 256, 8, 64)
    ns = np.stack([R[i]["ns"] for i in range(8)]).reshape(16, 2, 2, 4, 64, 128)
    return (y_prompt.astype(np.float32), y_sample.astype(np.float32), nk.astype(np.float32),
            nv.astype(np.float32), ns.astype(np.float32))
```

```python
from collections import deque
from contextlib import ExitStack

import os
import numpy as np
import concourse.bass as bass
import concourse.mybir as mybir
from concourse.bass_utils import run_bass_kernel_spmd

F32 = mybir.dt.float32
BF16 = mybir.dt.bfloat16
AF = mybir.ActivationFunctionType
ALU = mybir.AluOpType

D = 1024
KC = 8
DFF = 2816
FC = 22
DIN = 3104
EPS = 1e-6
NEG = -30000.0
NW = 4
SLOT = 4096


class Buf:
    __slots__ = ("name", "w", "r")

    def __init__(self, name=""):
        self.name = name
        self.w = None
        self.r = []


class V:
    __slots__ = ("ap", "bufs")

    def __init__(self, ap, bufs):
        self.ap = ap
        self.bufs = bufs if isinstance(bufs, (list, tuple)) else [bufs]

    def __getitem__(self, idx):
        return V(self.ap[idx], self.bufs)

    def re(self, pat, **kw):
        return V(self.ap.rearrange(pat, **kw), self.bufs)


def hk(h, k):
    if len(h.bufs) == 8:
        return V(h.ap[:, k, :], [h.bufs[k]])
    return h[:, k, :]


class TK:
    NDMA = 12

    def __init__(self, nc, es):
        self.nc = nc
        self.es = es
        self.eng = {"pe": nc.tensor, "act": nc.scalar, "dve": nc.vector, "pool": nc.gpsimd, "sp": nc.sync}
        self.semobj = {}
        self.cnt = {}
        self.waited = {e: {} for e in self.eng}
        for e in self.eng:
            self.semobj[e] = es.enter_context(nc.semaphore("s_" + e))
            self.cnt[e] = 0
        self.dma_idx = {}
        for q in ("sp", "pool"):
            self.dma_idx[q] = 0
            for i in range(self.NDMA):
                nm = f"d_{q}_{i}"
                self.semobj[nm] = es.enter_context(nc.semaphore(nm))
                self.cnt[nm] = 0
        self.pe_open = False
        self.nins = 0
        self.nwait = 0

    def sb(self, es, name, shape, dt=F32):
        self.nalloc = getattr(self, "nalloc", 0) + 1
        name = f"{name}_{self.nalloc}"
        t = es.enter_context(self.nc.sbuf_tensor(name, list(shape), dt))
        return V(t[:], Buf(name))

    def ps(self, es, name, shape, dt=F32):
        t = es.enter_context(self.nc.psum_tensor(name, list(shape), dt))
        return V(t[:], Buf(name))

    def _wait(self, e, sk, v):
        if self.waited[e].get(sk, 0) >= v:
            return
        self.eng[e].wait_ge(self.semobj[sk], v)
        self.waited[e][sk] = v
        self.nwait += 1

    def _deps(self, e, reads, writes):
        need = {}

        def add(idn):
            if idn is None:
                return
            sk, v = idn
            if need.get(sk, 0) < v:
                need[sk] = v

        for b in reads:
            add(b.w)
        for b in writes:
            add(b.w)
            for r in b.r:
                add(r)
        for sk, v in need.items():
            if sk == e:
                continue
            self._wait(e, sk, v)

    def _update(self, ident, reads, writes):
        for b in reads:
            b.r = [r for r in b.r if r[0] != ident[0]]
            b.r.append(ident)
        for b in writes:
            b.w = ident
            b.r = []

    def op(self, e, fn, reads, writes, inc=True):
        if e != "pe":
            assert not self.pe_open, "non-PE op emitted inside open PE group"
        rb = [b for v in reads for b in v.bufs]
        wb = [b for v in writes for b in v.bufs]
        self._deps(e, rb, wb)
        ins = fn()
        self.nins += 1
        if inc:
            self.cnt[e] += 1
            ins.then_inc(self.semobj[e], 1)
            ident = (e, self.cnt[e])
            if e == "pe":
                self.pe_open = False
        else:
            assert e == "pe"
            ident = (e, self.cnt[e] + 1)
            self.pe_open = True
        self._update(ident, rb, wb)
        return ins

    def dma(self, q, out, in_, reads=(), writes=(), **kw):
        assert not self.pe_open
        rb = [b for v in reads for b in v.bufs]
        wb = [b for v in writes for b in v.bufs]
        self._deps(q, rb, wb)
        i = self.dma_idx[q] % self.NDMA
        self.dma_idx[q] += 1
        nm = f"d_{q}_{i}"
        if self.cnt[nm]:
            self._wait(q, nm, self.cnt[nm])
        ins = self.eng[q].dma_start(out=out, in_=in_, **kw)
        self.cnt[nm] += 16
        ins.then_inc(self.semobj[nm], 16)
        self.nins += 1
        self._update((nm, self.cnt[nm]), rb, wb)
        return ins

    def barrier(self, engines=("pe", "act", "dve", "pool", "sp")):
        assert not self.pe_open
        for e in engines:
            for nm, c in self.cnt.items():
                if c and nm != e:
                    self._wait(e, nm, c)

    def mm(self, out, lhsT, rhs, start=True, stop=True, inc=None):
        if inc is None:
            inc = stop
        return self.op("pe", lambda: self.nc.tensor.matmul(out.ap, lhsT.ap, rhs.ap, start=start, stop=stop),
                       [lhsT, rhs], [out], inc=inc)

    def transpose(self, out, in_, ident):
        return self.op("pe", lambda: self.nc.tensor.transpose(out.ap, in_.ap, ident.ap), [in_, ident], [out])

    def act(self, out, in_, func, bias=None, scale=1.0):
        reads = [in_]
        kw = {}
        if bias is not None:
            if isinstance(bias, V):
                reads.append(bias)
                kw["bias"] = bias.ap
            else:
                kw["bias"] = bias
        if isinstance(scale, V):
            reads.append(scale)
            kw["scale"] = scale.ap
        else:
            kw["scale"] = scale
        return self.op("act", lambda: self.nc.scalar.activation(out=out.ap, in_=in_.ap, func=func, **kw),
                       reads, [out])

    def _e(self, e):
        return self.nc.vector if e == "dve" else self.nc.gpsimd

    def tt(self, e, out, a, b, op):
        return self.op(e, lambda: self._e(e).tensor_tensor(out.ap, a.ap, b.ap, op=op), [a, b], [out])

    def ts(self, e, out, a, s1, s2=None, op0=ALU.mult, op1=None):
        reads = [a]
        s1a = s1.ap if isinstance(s1, V) else s1
        s2a = s2.ap if isinstance(s2, V) else s2
        if isinstance(s1, V):
            reads.append(s1)
        if isinstance(s2, V):
            reads.append(s2)
        kw = {}
        if op1 is not None:
            kw["op1"] = op1
        return self.op(e, lambda: self._e(e).tensor_scalar(out.ap, a.ap, s1a, s2a, op0, **kw), reads, [out])

    def stt(self, e, out, in0, scalar, in1, op0, op1):
        reads = [in0, in1]
        sa = scalar.ap if isinstance(scalar, V) else scalar
        if isinstance(scalar, V):
            reads.append(scalar)
        return self.op(e, lambda: self._e(e).scalar_tensor_tensor(out.ap, in0.ap, sa, in1.ap, op0, op1),
                       reads, [out])

    def copy(self, e, out, in_):
        if e == "act":
            return self.op("act", lambda: self.nc.scalar.copy(out.ap, in_.ap), [in_], [out])
        return self.op(e, lambda: self._e(e).tensor_copy(out.ap, in_.ap), [in_], [out])

    def memset(self, e, out, val):
        return self.op(e, lambda: self._e(e).memset(out.ap, val), [], [out])

    def recip(self, out, in_):
        return self.op("dve", lambda: self.nc.vector.reciprocal(out.ap, in_.ap), [in_], [out])


def make_consts():
    c = {}
    c["identf"] = np.eye(128, dtype=np.float32)
    s = np.arange(128)[:, None]
    t = np.arange(128)[None, :]
    g = -1.0 / 16.0
    c["lmat"] = np.stack([(s <= t) * g, (s >= t) * g]).astype(np.float32)
    c["rmat"] = np.stack([(s > t) * g, (s < t) * g]).astype(np.float32)
    c["mask"] = np.stack([(s <= t) * 1.0, (s >= t) * 1.0]).astype(np.float32)
    rm = np.zeros((128, 128), np.float32)
    for p in range(128):
        i = p % 32
        if i < 16:
            rm[p + 16, p] = -1.0
        else:
            rm[p - 16, p] = 1.0
    c["ropem"] = rm
    tt_ = np.arange(1024)
    cos = np.zeros((128, 1024), np.float64)
    sin = np.zeros((128, 1024), np.float64)
    for p in range(128):
        d = p % 64
        blk = d // 32
        fi = (d % 32) % 16
        freq = np.float32(10000.0) ** (-np.float32(fi) / np.float32(16))
        pos = (tt_ // 64) if blk == 0 else (tt_ % 64)
        ang = pos.astype(np.float32) * np.float32(freq)
        cos[p] = np.cos(ang.astype(np.float64))
        sin[p] = np.sin(ang.astype(np.float64))
    c["ropecos"] = cos.astype(np.float32)
    c["ropesin"] = sin.astype(np.float32)
    ohk = np.zeros((128, 8, 128), np.float32)
    for kt in range(8):
        for i in range(2):
            for kc in range(64):
                ohk[2 * kt + i, kt, i * 64 + kc] = 1.0
                ohk[16 + kc, kt, i * 64 + kc] = 1.0
    pen = np.zeros((128, 1024), np.float32)
    for q in range(1024):
        qr, qc = q // 64, q % 64
        lo = min(max(qr - 4, 0), 8)
        ws = min(max(qc - 8, 0), 48)
        for r in range(16):
            if not (lo <= r < lo + 8):
                pen[r, q] = NEG
        for cc in range(64):
            if not (ws <= cc < ws + 16):
                pen[16 + cc, q] = NEG
    c["ohk"] = ohk.reshape(128, 1024)
    c["pen"] = pen
    c["j64"] = np.fliplr(np.eye(64, dtype=np.float32)).copy()
    c["zeros"] = np.zeros((136, 128), np.float32)
    return c


CONST_SHAPES = {"identf": [128, 128], "lmat": [2, 128, 128], "rmat": [2, 128, 128], "mask": [2, 128, 128],
                "ropem": [128, 128], "ropecos": [128, 1024], "ropesin": [128, 1024], "ohk": [128, 1024],
                "pen": [128, 1024], "j64": [64, 64], "zeros": [136, 128]}


class Kern:
    def __init__(self, nc):
        self.nc = nc
        self.kstop = float(os.environ.get("KSTOP", "9999"))
        self.stopped = False

    def ck(self, n):
        if n >= self.kstop:
            self.stopped = True
        return self.stopped

    def din(self, name, shape):
        return self.nc.dram_tensor(name, list(shape), F32, kind="ExternalInput").ap()

    def dout(self, name, shape):
        return self.nc.dram_tensor(name, list(shape), F32, kind="ExternalOutput").ap()

    def bank(self):
        b = self.bfree.popleft()
        self.bfree.append(b)
        return self.banks[b]

    def hold(self):
        b = self.bfree.popleft()
        return self.banks[b]

    def release(self, v):
        self.bfree.append([i for i in range(8) if self.banks[i] is v][0])

    def plan_weights(self):
        d = self.d
        plan = []

        def w2(ap2, c0, n):
            return ap2.rearrange("(k p) c -> p k c", p=128)[:, :, c0:c0 + n]

        for l in range(2):
            for b in range(18):
                plan.append((("mod", l, b), [(0, 8, 512, 0, 512, w2(d["w_mod"][l], b * 512, 512))]))
        for g in range(2):
            for l in range(2):
                def ffn(f):
                    for b in range(11):
                        plan.append((("fin", g, l, f, b), [
                            (0, 8, 512, 0, 256, w2(d["ffn_w_in"][l, f], b * 256, 256)),
                            (0, 8, 512, 256, 256, w2(d["ffn_w_in"][l, f], DFF + b * 256, 256))]))
                    for mp in range(4):
                        for kh in range(2):
                            src = d["ffn_w_out"][l, f].rearrange("(k p) c -> p k c", p=128)[
                                :, kh * 11:(kh + 1) * 11, mp * 256:(mp + 1) * 256]
                            plan.append((("fout", g, l, f, mp, kh), [(0, 11, 256, 0, 256, src)]))
                ffn(0)
                wi = d["w_in"][l]
                for bi, (c0, n) in enumerate([(0, 512), (512, 512), (1024, 512), (1536, 32), (1568, 512),
                                              (2080, 512), (2592, 512)]):
                    plan.append((("win", g, l, bi), [(0, 8, 512, 0, n, w2(wi, c0, n))]))
                for b in range(2):
                    plan.append((("wout", g, l, b), [(0, 8, 512, 0, 512, w2(d["w_out"][l], b * 512, 512))]))
                ffn(1)
        self.plan = plan
        self.ci = 0
        self.di = 0

    def _emit_w(self, j):
        tag, dmas = self.plan[j]
        slot = self.slots[j % NW]
        for di, (_, k, c, c0, n, src) in enumerate(dmas):
            dst = slot.ap[:, 0:k * c].rearrange("p (k c) -> p k c", k=k)[:, :, c0:c0 + n]
            wr = slot if len(dmas) == 1 else V(slot.ap, [slot.bufs[di]])
            self.tk.dma("pool", dst, src, writes=[wr])

    def wnext(self, tag):
        i = self.ci
        self.ci += 1
        while self.di < min(len(self.plan), i + NW):
            self._emit_w(self.di)
            self.di += 1
        assert self.plan[i][0] == tag, (self.plan[i][0], tag)
        k, c = self.plan[i][1][0][1], self.plan[i][1][0][2]
        slot = self.slots[i % NW]
        return V(slot.ap[:, 0:k * c].rearrange("p (k c) -> p k c", k=k), slot.bufs)

    def build(self):
        nc = self.nc
        d = {}
        d["xp"] = self.din("xp", [512, D])
        d["xs"] = self.din("xs", [1024, D])
        d["ck"] = self.din("ck", [2, 512, 512])
        d["cv"] = self.din("cv", [2, 512, 512])
        d["sg"] = self.din("sg", [2, 2, 256, 128])
        d["cvec"] = self.din("cvec", [16, 128])
        d["sel"] = self.din("sel", [128, 4])
        d["pen_own"] = self.din("pen_own", [128, 256])
        d["w_mod"] = self.din("w_mod", [2, D, 9 * D])
        d["b_mod"] = self.din("b_mod", [2, 72, 128])
        d["norm_g"] = self.din("norm_g", [96, 128])
        d["ffn_w_in"] = self.din("ffn_w_in", [2, 2, D, 2 * DFF])
        d["ffn_w_out"] = self.din("ffn_w_out", [2, 2, DFF, D])
        d["w_in"] = self.din("w_in", [2, D, DIN])
        d["gla_wa2"] = self.din("gla_wa2", [2, 2, 16, 256])
        d["gla_ba"] = self.din("gla_ba", [2, 2, 1, 256])
        d["gla_norm_g"] = self.din("gla_norm_g", [2, 128])
        d["nat_rpb"] = self.din("nat_rpb", [2, 8, 15, 31])
        d["w_out"] = self.din("w_out", [2, D, D])
        for k, s in CONST_SHAPES.items():
            d[k] = self.din("c_" + k, s)
        d["yp"] = self.dout("yp", [512, D])
        d["ys"] = self.dout("ys", [256, D])
        d["nk"] = self.dout("nk", [2, 2, 256, 512])
        d["nv"] = self.dout("nv", [2, 2, 256, 512])
        d["ns"] = self.dout("ns", [2, 2, 2, 256, 128])
        d["rp"] = nc.dram_tensor("rp_scr", [2, 8 * 17, 128], F32, kind="Internal").ap()
        self.d = d

        with ExitStack() as es:
            tk = self.tk = TK(nc, es)
            self.banks = [tk.ps(es, f"pb{i}", [128, 512]) for i in range(8)]
            self.bfree = deque(range(8))
            self.slots = [tk.sb(es, f"wslot{i}", [128, SLOT], BF16) for i in range(NW)]
            for sl_ in self.slots:
                sl_.bufs = [sl_.bufs[0], Buf("wslot_b")]
            self.plan_weights()
            self.consts(es)
            if not self.ck(0):
                self.modulation(es)
            if not self.ck(1):
                for g in range(2):
                    if not self.stopped:
                        self.run_group(g)
            assert self.stopped or self.ci == len(self.plan), (self.ci, len(self.plan))
            tk.barrier()
            self.stats = (tk.nins, tk.nwait, dict(tk.cnt))
        return nc

    def consts(self, es):
        tk, d = self.tk, self.d
        c = self.c = {}

        def ld(name, shape, src, dt=F32, q="sp"):
            v = tk.sb(es, "k_" + name, shape, dt)
            tk.dma(q, v.ap, src, writes=[v])
            c[name] = v
            return v

        ld("identf", [128, 128], d["identf"])
        ld("sel", [128, 4], d["sel"])
        for i in range(2):
            ld(f"lmat{i}", [128, 128], d["lmat"][i])
            ld(f"rmat{i}", [128, 128], d["rmat"][i])
            ld(f"mask{i}", [128, 128], d["mask"][i])
        ld("ropem", [128, 128], d["ropem"])
        ld("j64", [64, 64], d["j64"], BF16, "pool")
        ones = c["ones_d"] = tk.sb(es, "k_ones_d", [128, 128], BF16)
        tk.memset("dve", ones, 1.0 / 1024.0)
        ones = c["ones_v"] = tk.sb(es, "k_ones_v", [128, 128], BF16)
        tk.memset("dve", ones, 1.0 / 128.0)
        ones = c["ones1"] = tk.sb(es, "k_ones1", [128, 128], BF16)
        tk.memset("dve", ones, 1.0)
        self.sq = tk.sb(es, "sq", [128, 8, 512], BF16)
        self.tmp = [tk.sb(es, f"tmp{i}", [128, 512]) for i in range(3)]
        self.tmpi = 0
        self.rs = [tk.sb(es, f"rs{i}", [128, 512]) for i in range(2)]
        self.rsi = 0
        rpv = V(d["rp"], Buf("rp"))
        with ExitStack() as ls:
            for l in range(2):
                rt = tk.sb(ls, f"rpt{l}", [17, 8, 128])
                tk.memset("dve", rt, 0.0)
                tk.dma("sp", rt.ap[1:16, :, 48:79], d["nat_rpb"][l].rearrange("h a m -> a h m"), writes=[rt])
                tk.dma("sp", d["rp"][l].rearrange("(h a) m -> a h m", a=17), rt.ap, reads=[rt], writes=[rpv])
            tk.barrier()
        self.rpv = rpv

    def ntmp(self):
        self.tmpi = (self.tmpi + 1) % 3
        return self.tmp[self.tmpi]

    def nrs(self):
        self.rsi = (self.rsi + 1) % 2
        return self.rs[self.rsi]

    def modulation(self, es):
        tk, d, c = self.tk, self.d, self.c
        self.gn = tk.sb(es, "gn", [128, 2])
        self.coef = {(l, g): tk.sb(es, f"coef{l}{g}", [128, 9, 8]) for l in range(2) for g in range(2)}
        with ExitStack() as ls:
            st = tk.sb(ls, "mst", [96, 128])
            bm = [tk.sb(ls, f"mbm{l}", [72, 128]) for l in range(2)]
            cvs = tk.sb(ls, "mcv", [16, 128])
            gns = tk.sb(ls, "mgn", [2, 128])
            tk.dma("sp", st.ap, d["norm_g"], writes=[st])
            for l in range(2):
                tk.dma("sp", bm[l].ap, d["b_mod"][l], writes=[bm[l]])
            tk.dma("sp", cvs.ap, d["cvec"], writes=[cvs])
            tk.dma("sp", gns.ap, d["gla_norm_g"], writes=[gns])
            ngT = tk.sb(ls, "ngT", [128, 2, 6, 8])
            bmT = [tk.sb(ls, f"bmT{l}", [128, 72]) for l in range(2)]
            cT = tk.sb(ls, "cT", [128, 2, 8])
            p = self.bank()
            tk.transpose(p[:, 0:96], st, c["identf"][0:96, 0:96])
            tk.copy("dve", ngT.re("p l i k -> p (l i k)"), p[:, 0:96])
            for l in range(2):
                p = self.bank()
                tk.transpose(p[:, 0:72], bm[l], c["identf"][0:72, 0:72])
                tk.copy("dve", bmT[l], p[:, 0:72])
            p = self.bank()
            tk.transpose(p[:, 0:16], cvs, c["identf"][0:16, 0:16])
            tk.copy("dve", cT.re("p g k -> p (g k)"), p[:, 0:16])
            p = self.bank()
            tk.transpose(p[:, 0:2], gns, c["identf"][0:2, 0:2])
            tk.copy("dve", self.gn, p[:, 0:2])
            sc = tk.sb(ls, "sc", [128, 8, 2], BF16)
            for g in range(2):
                tk.act(sc[:, :, g], cT[:, g, :], AF.Silu)
            if self.ck(0.1):
                tk.barrier()
                return
            modT = [tk.sb(ls, f"modT{l}", [128, 72, 2]) for l in range(2)]
            for l in range(2):
                pm = self.hold()
                for b in range(18):
                    s3 = self.wnext(("mod", l, b))
                    for jj in range(4):
                        j = b * 4 + jj
                        for k in range(8):
                            tk.mm(pm[:, 2 * j:2 * j + 2], s3[:, k, jj * 128:(jj + 1) * 128], sc[:, k, :],
                                  start=(k == 0), stop=(k == 7))
                if self.ck(0.2 + l * 0.3):
                    tk.barrier()
                    return
                pm3 = pm[:, 0:144].re("p (j g) -> p j g", g=2)
                for g in range(2):
                    tk.tt("dve", modT[l][:, :, g], pm3[:, :, g], bmT[l], ALU.add)
                self.release(pm)
                for g in range(2):
                    cf = self.coef[(l, g)]
                    for s_ in range(3):
                        sh = modT[l][:, (3 * s_) * 8:(3 * s_ + 1) * 8, g]
                        scl = modT[l][:, (3 * s_ + 1) * 8:(3 * s_ + 2) * 8, g]
                        gt = modT[l][:, (3 * s_ + 2) * 8:(3 * s_ + 3) * 8, g]
                        tk.stt("dve", cf[:, 3 * s_ + 0, :], scl, 1.0, ngT[:, l, 2 * s_, :], ALU.add, ALU.mult)
                        tk.copy("dve", cf[:, 3 * s_ + 1, :], sh)
                        tk.stt("dve", cf[:, 3 * s_ + 2, :], gt, (1.0 if s_ == 1 else 0.5), ngT[:, l, 2 * s_ + 1, :],
                               ALU.mult, ALU.mult)
            tk.barrier()

    def rstd_of(self, src3, ones, W=512):
        tk = self.tk
        sq = self.sq[:, :, 0:W]
        tk.act(sq, src3, AF.Square)
        p = self.bank()
        for k in range(8):
            tk.mm(p[:, 0:W], ones, sq[:, k, :], start=(k == 0), stop=(k == 7))
        sd = self.ntmp()[:, 0:W]
        tk.act(sd, p[:, 0:W], AF.Sqrt, bias=EPS)
        r = self.nrs()[:, 0:W]
        tk.recip(r, sd)
        return r

    def prenorm(self, x, h, cf, s_, W=512):
        tk = self.tk
        r = self.rstd_of(x, self.c["ones_d"], W)
        for k in range(8):
            t = self.ntmp()[:, 0:W]
            tk.stt("dve", t, x[:, k, :], cf[:, 3 * s_, k:k + 1], r, ALU.mult, ALU.mult)
            tk.act(h[:, k, :], t, AF.Identity, bias=cf[:, 3 * s_ + 1, k:k + 1])

    def postnorm_res(self, y, x, cf, s_, W=512):
        tk = self.tk
        r = self.rstd_of(y, self.c["ones_d"], W)
        for k in range(8):
            t = self.ntmp()[:, 0:W]
            tk.stt("dve", t, y[:, k, :], cf[:, 3 * s_ + 2, k:k + 1], r, ALU.mult, ALU.mult)
            tk.tt("dve", x[:, k, :], x[:, k, :], t, ALU.add)

    def stats_a(self, src3, W, slot):
        tk = self.tk
        sq = self.sq[:, :, 0:W]
        tk.act(sq, src3, AF.Square)
        p = self.bank()
        for k in range(8):
            tk.mm(p[:, 0:W], self.c["ones_d"], sq[:, k, :], start=(k == 0), stop=(k == 7))
        sd = self.rs[slot][:, 0:W]
        tk.act(sd, p[:, 0:W], AF.Sqrt, bias=EPS)
        return sd

    def boundary(self, ys, tiles, cf, s_, nxt):
        tk = self.tk
        fused = {}
        if ys is not None:
            sds = [self.stats_a(ys[t], tiles[t][2], t) for t in range(len(tiles))]
            for sd in sds:
                tk.recip(sd, sd)
            fuse = nxt is not None and len(nxt[2]) == len(tiles) and all(
                nxt[2][t][0] is tiles[t][0] for t in range(len(tiles)))
            for t, (x, h, W) in enumerate(tiles):
                p = self.bank() if fuse else None
                for k in range(8):
                    tm = self.ntmp()[:, 0:W]
                    tk.stt("dve", tm, ys[t][:, k, :], cf[:, 3 * s_ + 2, k:k + 1], sds[t], ALU.mult, ALU.mult)
                    tk.tt("dve", x[:, k, :], x[:, k, :], tm, ALU.add)
                    if fuse:
                        tk.act(self.sq[:, k, 0:W], x[:, k, :], AF.Square)
                        tk.mm(p[:, 0:W], self.c["ones_d"], self.sq[:, k, 0:W], start=(k == 0), stop=(k == 7),
                              inc=True)
                if fuse:
                    fused[t] = p
        if nxt is not None:
            ncf, ns_, ntiles = nxt
            for t, (x, h, W) in enumerate(ntiles):
                if t in fused:
                    sdx_t = self.rs[t][:, 0:W]
                    tk.act(sdx_t, fused[t][:, 0:W], AF.Sqrt, bias=EPS)
                else:
                    sdx_t = self.stats_a(x, W, t)
                tk.recip(sdx_t, sdx_t)
                for k in range(8):
                    tm = self.ntmp()[:, 0:W]
                    tk.stt("dve", tm, x[:, k, :], ncf[:, 3 * ns_, k:k + 1], sdx_t, ALU.mult, ALU.mult)
                    tk.act(hk(h, k), tm, AF.Identity, bias=ncf[:, 3 * ns_ + 1, k:k + 1])

    def run_group(self, g):
        tk, d, c = self.tk, self.d, self.c
        ntl = 1 if g == 0 else 2
        self.g, self.ntl = g, ntl
        src = d["xp"] if g == 0 else d["xs"]
        dst = d["yp"] if g == 0 else d["ys"]
        with ExitStack() as gs:
            self.xT = [tk.sb(gs, f"xT{t}", [128, 8, 512]) for t in range(ntl)]
            self.hT = [tk.sb(gs, f"hT{t}", [128, 8, 512], BF16) for t in range(ntl)]
            for h_ in self.hT:
                h_.bufs = [Buf(f"h{k}") for k in range(8)]
            if g == 1:
                self.xo = V(self.hT[1].ap.bitcast(F32), self.hT[1].bufs)
            ss = ExitStack()
            stage = tk.sb(ss, "stage", [128, 4, 1024])
            for t in range(ntl):
                tk.dma("sp", stage.ap, src[t * 512:(t + 1) * 512, :].rearrange("(i p) f -> p i f", p=128),
                       writes=[stage])
                for k in range(8):
                    p = self.bank()
                    for i in range(4):
                        tk.transpose(p[:, i * 128:(i + 1) * 128], stage[:, i, k * 128:(k + 1) * 128], c["identf"])
                    tk.copy("act" if k % 2 else "dve", self.xT[t][:, k, :], p)
            tk.barrier()
            ss.close()
            self.ck(100 * g + 2)
            subs = [(l, kind) for l in range(2) for kind in (0, 1, 2)]
            own_tiles = [(self.xo, self.hT[0][:, :, 0:256], 256)] if g == 1 else None
            for si_, (l, kind) in enumerate(subs):
                if self.stopped:
                    break
                cf = self.coef[(l, g)]
                full_tiles = [(self.xT[t], self.hT[t], 512) for t in range(ntl)]
                nxt = None
                if si_ + 1 < len(subs):
                    nl, nkind = subs[si_ + 1]
                    ntiles = own_tiles if (g == 1 and (nl, nkind) == (1, 2)) else full_tiles
                    nxt = (self.coef[(nl, g)], nkind, ntiles)
                pre = (si_ == 0)
                if kind == 1:
                    self.mixer(l, cf, do_pre=pre, nxt=nxt)
                elif g == 1 and (l, kind) == (1, 2):
                    self.ffn(l, 1, cf, 2, tiles=own_tiles, do_pre=pre, nxt=nxt)
                else:
                    self.ffn(l, 0 if kind == 0 else 1, cf, kind, do_pre=pre, nxt=nxt)
                self.ck(100 * g + 10 * (kind + 1) + 30 * l)
            stage = tk.sb(gs, "stage", [128, 4, 1024])
            if g == 0:
                for t in range(ntl):
                    for i in range(4):
                        for hf in range(2):
                            p = self.bank()
                            for kk in range(4):
                                tk.transpose(p[:, kk * 128:(kk + 1) * 128],
                                             self.xT[t][:, hf * 4 + kk, i * 128:(i + 1) * 128], c["identf"])
                            tk.copy("act" if hf else "dve", stage[:, i, hf * 512:(hf + 1) * 512], p)
                    tk.dma("sp", dst[t * 512:(t + 1) * 512, :].rearrange("(i p) f -> p i f", p=128), stage.ap,
                           reads=[stage])
            else:
                for i in range(2):
                    for hf in range(2):
                        p = self.bank()
                        for kk in range(4):
                            tk.transpose(p[:, kk * 128:(kk + 1) * 128],
                                         self.xo[:, hf * 4 + kk, i * 128:(i + 1) * 128], c["identf"])
                        tk.copy("act" if hf else "dve", stage[:, i, hf * 512:(hf + 1) * 512], p)
                tk.dma("sp", dst.rearrange("(i p) f -> p i f", p=128), stage.ap[:, 0:2, :], reads=[stage])
            tk.barrier()

    def ffn(self, l, f, cf, s_, tiles=None, do_pre=True, nxt=None):
        tk = self.tk
        g = self.g
        if tiles is None:
            tiles = [(self.xT[t], self.hT[t], 512) for t in range(self.ntl)]
        ntl = len(tiles)
        with ExitStack() as fs:
            big = [tk.sb(fs, f"big{t}", [128, FC, tiles[t][2]], BF16) for t in range(ntl)]
            yT = [tk.sb(fs, f"yT{t}", [128, 8, tiles[t][2]]) for t in range(ntl)]
            if do_pre:
                self.boundary(None, None, None, None, (cf, s_, tiles))
            for b in range(11):
                s3 = self.wnext(("fin", g, l, f, b))
                for jj in range(2):
                    j = 2 * b + jj
                    for t in range(ntl):
                        x_, h_, W = tiles[t]
                        pa = self.bank()
                        pb = self.bank()
                        for k in range(8):
                            tk.mm(pa[:, 0:W], s3[:, k, jj * 128:(jj + 1) * 128], hk(h_, k),
                                  start=(k == 0), stop=(k == 7))
                        for k in range(8):
                            tk.mm(pb[:, 0:W], s3[:, k, 256 + jj * 128:256 + (jj + 1) * 128], hk(h_, k),
                                  start=(k == 0), stop=(k == 7))
                        tm = self.ntmp()[:, 0:W]
                        tk.act(tm, pa[:, 0:W], AF.Silu)
                        tk.tt("dve", big[t][:, j, :], tm, pb[:, 0:W], ALU.mult)
            ev = 0
            for mp in range(4):
                pbs = [[self.hold() for t in range(ntl)] for mm_ in range(2)]
                for kh in range(2):
                    s3 = self.wnext(("fout", g, l, f, mp, kh))
                    for mm_ in range(2):
                        for t in range(ntl):
                            W = tiles[t][2]
                            p = pbs[mm_][t]
                            for k in range(11):
                                tk.mm(p[:, 0:W], s3[:, k, mm_ * 128:(mm_ + 1) * 128], big[t][:, kh * 11 + k, :],
                                      start=(kh == 0 and k == 0), stop=(kh == 1 and k == 10), inc=(k == 10))
                for mm_ in range(2):
                    for t in range(ntl):
                        W = tiles[t][2]
                        tk.copy("act" if ev % 2 else "dve", yT[t][:, 2 * mp + mm_, :], pbs[mm_][t][:, 0:W])
                        ev += 1
                        self.release(pbs[mm_][t])
            self.boundary(yT, tiles, cf, s_, nxt)
            tk.barrier()

    def mixer(self, l, cf, do_pre=True, nxt=None):
        tk, d, c = self.tk, self.d, self.c
        g, ntl = self.g, self.ntl
        T = 512 * ntl
        nt = 4 * ntl
        lat = (g == 1)
        seqs = [list(range(nt))] if lat else [[0, 1], [2, 3]]
        hT = self.hT

        def hcols(i):
            t, ii = divmod(i, 4)
            return [hT[t][:, k, ii * 128:(ii + 1) * 128] for k in range(8)]

        with ExitStack() as ms:
            if do_pre:
                self.boundary(None, None, None, None, (cf, 1, [(self.xT[t], hT[t], 512) for t in range(ntl)]))
            mixT = tk.sb(ms, "mixT", [128, 8, T], BF16)
            own = lat and l == 1
            with ExitStack() as gs:
                sr = tk.sb(gs, "sr", [128, 4, T], BF16)
                vtok = tk.sb(gs, "vtok", [128, nt, 512], BF16)
                qs = [[tk.sb(gs, f"qs{i}{hp}", [128, T], BF16) for hp in range(2)] for i in range(2)]
                ks = [[tk.sb(gs, f"ks{i}{hp}", [128, T], BF16) for hp in range(2)] for i in range(2)]
                kd = [tk.sb(gs, f"kd{i}", [128, nt, 256], BF16) for i in range(2)]
                dec = [[tk.sb(gs, f"dec{i}{hp}", [128, nt]) for hp in range(2)] for i in range(2)]
                pa = ExitStack()
                qT = tk.sb(pa, "qT", [128, 2, T])
                kT = tk.sb(pa, "kT", [128, 2, T])
                lrA = [tk.sb(pa, f"lrA{i}", [32, T], BF16) for i in range(2)]
                ktok = tk.sb(pa, "ktok", [128, nt, 256])
                wa2a = [tk.sb(pa, f"wa2a{i}", [17, 256], BF16) for i in range(2)]
                gpb = [tk.sb(pa, f"gp{i}", [128, 256]) for i in range(2)]
                e1b = [tk.sb(pa, f"e1{i}", [128, 256]) for i in range(2)]
                ETb = [tk.sb(pa, f"ET{i}", [128, 128]) for i in range(4)]
                EIb = [tk.sb(pa, f"EI{i}", [128, 128]) for i in range(4)]
                ERb = [tk.sb(pa, f"ER{i}", [128, 256]) for i in range(2)]
                if lat:
                    rcos = tk.sb(pa, "rcos", [128, 1024])
                    rsin = tk.sb(pa, "rsin", [128, 1024])
                    tk.dma("sp", rcos.ap, d["ropecos"], writes=[rcos])
                    tk.dma("sp", rsin.ap, d["ropesin"], writes=[rsin])
                for i in range(2):
                    tk.dma("pool", wa2a[i].ap[0:16, :], d["gla_wa2"][l, i], writes=[wa2a[i]])
                    tk.dma("pool", wa2a[i].ap[16:17, :], d["gla_ba"][l, i], writes=[wa2a[i]])
                    tk.memset("dve", lrA[i], 1.0)
                s3 = self.wnext(("win", g, l, 0))
                for cch in range(4):
                    for t in range(ntl):
                        p = self.bank()
                        for k in range(8):
                            tk.mm(p, s3[:, k, cch * 128:(cch + 1) * 128], hT[t][:, k, :], start=(k == 0), stop=(k == 7))
                        dstv = (qT if cch < 2 else kT)[:, cch % 2, t * 512:(t + 1) * 512]
                        tk.copy("act" if (cch + t) % 2 else "dve", dstv, p)
                for i in range(nt if not lat else 0):
                    p = self.bank()
                    hc = hcols(i)
                    for k in range(8):
                        tk.mm(p[:, 0:256], hc[k], s3[:, k, 256:512], start=(k == 0), stop=(k == 7))
                    tk.copy("act" if i % 2 else "dve", ktok[:, i, :], p[:, 0:256])
                s3 = self.wnext(("win", g, l, 1))
                for i in range(nt):
                    p = self.bank()
                    hc = hcols(i)
                    for k in range(8):
                        tk.mm(p, hc[k], s3[:, k, :], start=(k == 0), stop=(k == 7))
                    tk.copy("act" if i % 2 else "dve", vtok[:, i, :], p)
                s3 = self.wnext(("win", g, l, 2))
                for cch in range(4):
                    for t in range(ntl):
                        p = self.bank()
                        for k in range(8):
                            tk.mm(p, s3[:, k, cch * 128:(cch + 1) * 128], hT[t][:, k, :], start=(k == 0), stop=(k == 7))
                        tk.act(sr[:, cch, t * 512:(t + 1) * 512], p, AF.Silu)
                s3 = self.wnext(("win", g, l, 3))
                for i in range(2):
                    for t in range(ntl):
                        p = self.bank()
                        for k in range(8):
                            tk.mm(p[0:16, :], s3[:, k, i * 16:(i + 1) * 16], hT[t][:, k, :], start=(k == 0), stop=(k == 7))
                        tk.copy("dve", lrA[i][0:16, t * 512:(t + 1) * 512], p[0:16, :])
                if lat:
                    for xx in (qT, kT):
                        xx.bufs = [xx.bufs[0]] + [Buf("rope") for _ in range(2 * ntl - 1)]
                    tk.barrier(("pe", "dve", "act"))
                    for xx in (qT, kT):
                        for hp in range(2):
                            for t in range(ntl):
                                cs = slice(t * 512, (t + 1) * 512)
                                xv = V(xx.ap[:, hp, cs], [xx.bufs[hp * ntl + t]])
                                p = self.bank()
                                tk.mm(p, c["ropem"], xv)
                                t1 = self.ntmp()
                                tk.tt("dve", t1, p, rsin[:, cs], ALU.mult)
                                tk.tt("dve", xv, xv, rcos[:, cs], ALU.mult)
                                tk.tt("dve", xv, xv, t1, ALU.add)
                if lat:
                    for i in range(nt):
                        p = self.bank()
                        for hp in range(2):
                            tk.transpose(p[:, hp * 128:(hp + 1) * 128], kT[:, hp, i * 128:(i + 1) * 128], c["identf"])
                        tk.copy("act" if i % 2 else "dve", ktok[:, i, :], p[:, 0:256])
                gitems = [(i, dr) for i in range(nt) for dr in range(2)]

                def G1(n):
                    i, dr = gitems[n]
                    cs = slice(i * 128, (i + 1) * 128)
                    pz = self.bank()
                    tk.mm(pz[:, 0:256], lrA[dr][0:17, cs], wa2a[dr][0:17, :])
                    tk.act(e1b[dr], pz[:, 0:256], AF.Exp, scale=-1.0)
                    tk.act(gpb[dr], e1b[dr], AF.Ln, bias=1.0)

                def G2(n):
                    i, dr = gitems[n]
                    cs = slice(i * 128, (i + 1) * 128)
                    gp = gpb[dr]
                    pcs = []
                    for hp in range(2):
                        pc = self.bank()
                        tk.mm(pc[:, 0:128], gp[:, hp * 128:(hp + 1) * 128], c[f"lmat{dr}"])
                        pcs.append(pc)
                    pr = self.bank()
                    tk.mm(pr[:, 0:256], c[f"rmat{dr}"], gp)
                    for hp in range(2):
                        ET, EI = ETb[2 * dr + hp], EIb[2 * dr + hp]
                        tk.act(ET, pcs[hp][:, 0:128], AF.Exp)
                        tk.act(EI, pcs[hp][:, 0:128], AF.Exp, scale=-1.0)
                        tk.stt("dve", qs[dr][hp][:, cs], qT[:, hp, cs], 0.125, ET, ALU.mult, ALU.mult)
                        tk.tt("dve", ks[dr][hp][:, cs], kT[:, hp, cs], EI, ALU.mult)
                        lc = 127 if dr == 0 else 0
                        tk.copy("dve", dec[dr][hp][:, i:i + 1], ET[:, lc:lc + 1])
                    ER = ERb[dr]
                    tk.act(ER, pr[:, 0:256], AF.Exp)
                    tk.tt("dve", kd[dr][:, i, :], ktok[:, i, :], ER, ALU.mult)

                G1(0)
                for n in range(len(gitems)):
                    if n + 1 < len(gitems):
                        G1(n + 1)
                    G2(n)
                tk.barrier()
                pa.close()
                S = [[tk.sb(gs, f"S{i}{hp}", [128, 128]) for hp in range(2)] for i in range(2)]
                Sb = [[[tk.sb(gs, f"Sb{i}{hp}{j}", [128, 128], BF16) for j in range(nt)] for hp in range(2)]
                      for i in range(2)]
                smb = [tk.sb(gs, f"sm{i}", [128, 128], BF16) for i in range(8)]
                osqb = [tk.sb(gs, f"osq{i}", [128, 512], BF16) for i in range(2)]
                for si, tiles in enumerate(seqs):
                    chains = [(dr, hp) for dr in range(2) for hp in range(2)]
                    for dr, hp in chains:
                        St = S[dr][hp]
                        if lat:
                            tk.dma("sp", St.ap, d["sg"][l, dr, hp * 128:(hp + 1) * 128, :], writes=[St])
                        else:
                            tk.memset("dve", St, 0.0)
                    for step in range(len(tiles)):
                        for dr, hp in chains:
                            St = S[dr][hp]
                            i = tiles[step] if dr == 0 else tiles[-1 - step]
                            tk.copy("act", Sb[dr][hp][i], St)
                            pk = self.bank()
                            for e in range(2):
                                h = 2 * hp + e
                                tk.mm(pk[e * 64:(e + 1) * 64, 0:128], kd[dr][:, i, h * 64:(h + 1) * 64],
                                      vtok[:, i, h * 128:(h + 1) * 128])
                            tk.stt("dve", St, St, dec[dr][hp][:, i:i + 1], pk[:, 0:128], ALU.mult, ALU.add)
                    if not lat:
                        for dr, hp in chains:
                            tk.dma("sp", d["ns"][si, l, dr, hp * 128:(hp + 1) * 128, :], S[dr][hp].ap,
                                   reads=[S[dr][hp]])
                oitems = [(blk, h, ii) for blk in range(ntl) for h in range(4) for ii in range(4)]
                OLA = 2
                smd, pos = {}, {}
                smi = [0]

                def P1(it):
                    blk, h, ii = it
                    hp, e = divmod(h, 2)
                    es_ = slice(e * 64, (e + 1) * 64)
                    i = blk * 4 + ii
                    cs = slice(i * 128, (i + 1) * 128)
                    sms = []
                    for dr in range(2):
                        psc = self.bank()
                        tk.mm(psc[:, 0:128], ks[dr][hp][es_, cs], qs[dr][hp][es_, cs])
                        sm = smb[smi[0] % len(smb)]
                        smi[0] += 1
                        tk.tt("dve", sm, psc[:, 0:128], c[f"mask{dr}"], ALU.mult)
                        sms.append(sm)
                    smd[it] = sms

                def P2(it):
                    blk, h, ii = it
                    hp, e = divmod(h, 2)
                    es_ = slice(e * 64, (e + 1) * 64)
                    i = blk * 4 + ii
                    cs = slice(i * 128, (i + 1) * 128)
                    if ii == 0:
                        pos[(blk, h)] = self.hold()
                    po = pos[(blk, h)]
                    sms = smd.pop(it)
                    oc = po[:, ii * 128:(ii + 1) * 128]
                    tk.mm(oc, vtok[:, i, h * 128:(h + 1) * 128], sms[0], start=True, stop=False)
                    tk.mm(oc, Sb[0][hp][i][es_, :], qs[0][hp][es_, cs], start=False, stop=False)
                    tk.mm(oc, vtok[:, i, h * 128:(h + 1) * 128], sms[1], start=False, stop=False)
                    tk.mm(oc, Sb[1][hp][i][es_, :], qs[1][hp][es_, cs], start=False, stop=True)
                    if ii == 3:
                        osq = osqb[(blk * 4 + h) % 2]
                        tk.act(osq, po, AF.Square)
                        p2 = self.bank()
                        tk.mm(p2, c["ones_v"], osq)
                        sd = self.ntmp()
                        tk.act(sd, p2, AF.Sqrt, bias=EPS)
                        r = self.nrs()
                        tk.recip(r, sd)
                        t1 = self.ntmp()
                        tk.stt("dve", t1, po, self.gn[:, l:l + 1], r, ALU.mult, ALU.mult)
                        tk.tt("dve", mixT[:, h, blk * 512:(blk + 1) * 512], t1, sr[:, h, blk * 512:(blk + 1) * 512],
                              ALU.mult)
                        self.release(po)

                for n in range(len(oitems) + OLA):
                    if n < len(oitems):
                        P1(oitems[n])
                    if n >= OLA:
                        P2(oitems[n - OLA])
                tk.barrier()
            if self.ck(100 * g + 15 + 30 * l):
                return
            mixo = tk.sb(ms, "mixo", [128, 8, 256], BF16) if own else None
            with ExitStack() as ns:
                qn = tk.sb(ns, "qn", [128, 4, T], BF16)
                kn = tk.sb(ns, "kn", [128, 4, T], BF16)
                vn = tk.sb(ns, "vn", [128, nt, 512], BF16)
                exb = [tk.sb(ns, f"ex{i}", [128, 512], BF16) for i in range(6)]
                exi = 0
                rden = tk.sb(ns, "rden", [128, 512])
                if not lat:
                    self.stage = tk.sb(ns, "stage", [128, 4, 1024])
                s3 = self.wnext(("win", g, l, 4))
                for cch in range(4):
                    for t in range(ntl):
                        p = self.bank()
                        for k in range(8):
                            tk.mm(p, s3[:, k, cch * 128:(cch + 1) * 128], hT[t][:, k, :], start=(k == 0), stop=(k == 7))
                        tk.act(qn[:, cch, t * 512:(t + 1) * 512], p, AF.Copy, scale=0.125)
                s3 = self.wnext(("win", g, l, 5))
                for cch in range(4):
                    for t in range(ntl):
                        p = self.bank()
                        for k in range(8):
                            tk.mm(p, s3[:, k, cch * 128:(cch + 1) * 128], hT[t][:, k, :], start=(k == 0), stop=(k == 7))
                        tk.copy("act" if (cch + t) % 2 else "dve", kn[:, cch, t * 512:(t + 1) * 512], p)
                if not lat:
                    for i in range(nt):
                        p = self.bank()
                        hc = hcols(i)
                        for k in range(8):
                            tk.mm(p, hc[k], s3[:, k, :], start=(k == 0), stop=(k == 7))
                        tk.copy("dve", self.stage[:, i, 0:512], p)
                    for si in range(2):
                        tk.dma("sp", d["nk"][si, l].rearrange("(i p) f -> p i f", p=128),
                               self.stage.ap[:, 2 * si:2 * si + 2, 0:512], reads=[self.stage])
                s3 = self.wnext(("win", g, l, 6))
                for i in range(nt):
                    p = self.bank()
                    hc = hcols(i)
                    for k in range(8):
                        tk.mm(p, hc[k], s3[:, k, :], start=(k == 0), stop=(k == 7))
                    tk.copy("act", vn[:, i, :], p)
                    if not lat:
                        tk.copy("dve", self.stage[:, i, 512:1024], p)
                if not lat:
                    for si in range(2):
                        tk.dma("sp", d["nv"][si, l].rearrange("(i p) f -> p i f", p=128),
                               self.stage.ap[:, 2 * si:2 * si + 2, 512:1024], reads=[self.stage])
                    citems = [(hp, e, si) for hp in range(4) for e in range(2) for si in range(2)]
                    CLA = 2
                    cex, cpods = {}, {}
                    exn = [0]

                    def cA(it):
                        hp, e, si = it
                        es_ = slice(e * 64, (e + 1) * 64)
                        qsl = slice(si * 256, (si + 1) * 256)
                        psc = self.bank()
                        for kt in range(2):
                            i = 2 * si + kt
                            tk.mm(psc[:, kt * 256:(kt + 1) * 256], kn[es_, hp, i * 128:(i + 1) * 128], qn[es_, hp, qsl])
                        ex = exb[exn[0] % len(exb)]
                        exn[0] += 1
                        tk.act(ex, psc, AF.Exp)
                        cex[it] = ex

                    def cC(it):
                        hp, e, si = it
                        h = 2 * hp + e
                        es_ = slice(e * 64, (e + 1) * 64)
                        qsl = slice(si * 256, (si + 1) * 256)
                        if e == 0 and si == 0:
                            cpods[hp] = (self.hold(), self.hold())
                        po, pd = cpods[hp]
                        ex = cex.pop(it)
                        for kt in range(2):
                            i = 2 * si + kt
                            tk.mm(po[es_, qsl], vn[:, i, h * 64:(h + 1) * 64], ex[:, kt * 256:(kt + 1) * 256],
                                  start=(kt == 0), stop=(kt == 1), inc=True)
                        for kt in range(2):
                            tk.mm(pd[es_, qsl], c["ones1"][:, 0:64], ex[:, kt * 256:(kt + 1) * 256],
                                  start=(kt == 0), stop=(kt == 1), inc=True)
                        if e == 1 and si == 1:
                            tk.recip(rden, pd)
                            tk.tt("dve", mixT[:, 4 + hp, :], po, rden, ALU.mult)
                            self.release(po)
                            self.release(pd)

                    for n in range(len(citems) + CLA):
                        if n < len(citems):
                            cA(citems[n])
                        if n >= CLA:
                            cC(citems[n - CLA])
                else:
                    ckT = tk.sb(ns, "ckT", [128, 4, 512], BF16)
                    cvt = tk.sb(ns, "cvt", [128, 4, 512], BF16)
                    c["ohk"] = tk.sb(ns, "ohk", [128, 1024], BF16)
                    c["pen"] = tk.sb(ns, "pen", [128, 1024], BF16)
                    tk.dma("pool", c["ohk"].ap, d["ohk"], writes=[c["ohk"]])
                    tk.dma("pool", c["pen"].ap, d["pen"], writes=[c["pen"]])
                    Bt = tk.sb(ns, "Bt", [128, 8, 16, 64], BF16)
                    tk.dma("pool", cvt.ap, d["cv"][l].rearrange("(a p) f -> p a f", p=128), writes=[cvt])
                    with ExitStack() as cks:
                        ckf = tk.sb(cks, "ckf", [128, 4, 512])
                        tk.dma("sp", ckf.ap, d["ck"][l].rearrange("(a p) f -> p a f", p=128), writes=[ckf])
                        for a in range(4):
                            p = self.bank()
                            for hp in range(4):
                                tk.transpose(p[:, hp * 128:(hp + 1) * 128], ckf[:, a, hp * 128:(hp + 1) * 128],
                                             c["identf"])
                            tk.copy("dve", ckT.re("p h (a k) -> p h a k", a=4)[:, :, a, :],
                                    p.re("p (h k) -> p h k", h=4))
                        tk.barrier()
                    with ExitStack() as tts:
                        TT = tk.sb(tts, "TT", [64, 8 * 17 * 64], BF16)
                        rp = d["rp"]
                        TTh = TT.ap.rearrange("p (h a c) -> p h a c", h=8, a=17)
                        ttb = [Buf(f"tt{h8}") for h8 in range(8)]
                        for h8 in range(8):
                            srcap = bass.AP(tensor=rp.tensor, offset=(l * 8 + h8) * 17 * 128, ap=[[1, 64], [128, 17], [1, 64]])
                            tk.dma("pool", TTh[:, h8], srcap, reads=[self.rpv], writes=[V(TT.ap, [ttb[h8]])])
                        TT4 = TT.re("p (h a c) -> p h a c", h=8, a=17)
                        for h8 in range(8):
                            for grp in range(2):
                                p = self.bank()
                                for ii in range(8):
                                    idx = grp * 8 + ii
                                    lh = V(TT4.ap[:, h8, 15 - idx:17 - idx, :].rearrange("p a c -> p (a c)"), [ttb[h8]])
                                    tk.mm(p[:, ii * 64:(ii + 1) * 64], lh, c["j64"])
                                tk.copy("act" if grp else "dve",
                                        Bt[:, h8, grp * 8:(grp + 1) * 8, :].re("p a c -> p (a c)"), p)
                        tk.barrier()
                    ohk3 = c["ohk"].re("p (t k) -> p t k", t=8)
                    if self.ck(100 * g + 17 + 30 * l):
                        tk.barrier()
                        return
                    if own:
                        sel = c["sel"]
                        qno = tk.sb(ns, "qno", [128, 4, 256], BF16)
                        peno = tk.sb(ns, "peno", [128, 256], BF16)
                        tk.dma("pool", peno.ap, d["pen_own"], writes=[peno])
                        for r in range(4):
                            qsl_ = qn[:, :, r * 256:(r + 1) * 256]
                            if r == 0:
                                tk.ts("dve", qno, qsl_, sel[:, 0:1], None, op0=ALU.mult)
                            else:
                                tk.stt("dve", qno, qsl_, sel[:, r:r + 1], qno, ALU.mult, ALU.add)
                        pos_ = [self.hold(), self.hold()]
                        pds_ = [self.hold(), self.hold()]
                        oitems_ = [(kt, hp, e) for hp in range(4) for e in range(2) for kt in range(12)]
                        OLA_ = int(os.environ.get("KOLA", "3"))
                        opsc, oexs = {}, {}
                        ocnt = [0]

                        def oA(it):
                            kt, hp, e = it
                            h = 2 * hp + e
                            es_ = slice(e * 64, (e + 1) * 64)
                            psc = self.bank()
                            if kt < 8:
                                tk.mm(psc[:, 0:256], kn[es_, hp, kt * 128:(kt + 1) * 128], qno[es_, hp, :],
                                      start=True, stop=False)
                                tk.mm(psc[:, 0:256], ohk3[:, kt, :], peno, start=False, stop=True)
                                for r in range(4):
                                    i0_ = 7 - 2 * kt + 4 * r
                                    jlo = max(0, -i0_)
                                    jhi = min(4, 16 - i0_)
                                    if jlo < jhi:
                                        src_ = Bt[:, h, i0_ + jlo:i0_ + jhi, :].re("p a c -> p (a c)")
                                        pv_ = psc[:, jlo * 64:jhi * 64]
                                        tk.stt("dve", pv_, src_, sel[:, r:r + 1], pv_, ALU.mult, ALU.add)
                            else:
                                a = kt - 8
                                tk.mm(psc[:, 0:256], ckT[es_, hp, a * 128:(a + 1) * 128], qno[es_, hp, :])
                            opsc[it] = psc

                        def oB(it):
                            ex = exb[ocnt[0] % len(exb)]
                            ocnt[0] += 1
                            tk.act(ex[:, 0:256], opsc.pop(it)[:, 0:256], AF.Exp)
                            oexs[it] = ex

                        def oC(it):
                            kt, hp, e = it
                            h = 2 * hp + e
                            es_ = slice(e * 64, (e + 1) * 64)
                            csl = slice((hp % 2) * 256, (hp % 2 + 1) * 256)
                            po, pd = pos_[hp // 2], pds_[hp // 2]
                            vv = vn[:, kt, h * 64:(h + 1) * 64] if kt < 8 else cvt[:, kt - 8, h * 64:(h + 1) * 64]
                            ex = oexs.pop(it)
                            tk.mm(po[es_, csl], vv, ex[:, 0:256], start=(kt == 0), stop=(kt == 11), inc=True)
                            tk.mm(pd[es_, csl], c["ones1"][:, 0:64], ex[:, 0:256], start=(kt == 0), stop=(kt == 11),
                                  inc=True)

                        for n in range(len(oitems_) + OLA_):
                            if n < len(oitems_):
                                oA(oitems_[n])
                            if n >= OLA_:
                                oB(oitems_[n - OLA_])
                                oC(oitems_[n - OLA_])
                        for hp2 in range(2):
                            tk.recip(rden, pds_[hp2])
                            tk.tt("dve", mixo[:, 4 + 2 * hp2:6 + 2 * hp2, :],
                                  pos_[hp2].re("p (a c) -> p a c", a=2), rden.re("p (a c) -> p a c", a=2), ALU.mult)
                            self.release(pos_[hp2])
                            self.release(pds_[hp2])
                    else:
                        kts = {0: [0, 1, 2, 3, 4, 5, 8, 9, 10, 11], 1: [2, 3, 4, 5, 6, 7, 8, 9, 10, 11]}
                        items = [(hp, Q, e, kt) for hp in range(4) for Q in range(2) for e in range(2)
                                 for kt in kts[Q]]
                        LA = 3
                        pscs, exs, pods = {}, {}, {}

                        def stA(it):
                            hp, Q, e, kt = it
                            h = 2 * hp + e
                            es_ = slice(e * 64, (e + 1) * 64)
                            qsl = slice(Q * 512, (Q + 1) * 512)
                            psc = self.bank()
                            if kt < 8:
                                tk.mm(psc, kn[es_, hp, kt * 128:(kt + 1) * 128], qn[es_, hp, qsl], start=True, stop=False)
                                tk.mm(psc, ohk3[:, kt, :], c["pen"][:, qsl], start=False, stop=True)
                                jlo = max(0, 2 * kt - 8 * Q - 7)
                                jhi = min(8, 2 * kt - 8 * Q + 9)
                                if jlo < jhi:
                                    ilo = 7 - 2 * kt + 8 * Q + jlo
                                    bsl = Bt[:, h, ilo:ilo + (jhi - jlo), :].re("p a c -> p (a c)")
                                    tk.tt("dve", psc[:, jlo * 64:jhi * 64], psc[:, jlo * 64:jhi * 64], bsl, ALU.add)
                            else:
                                a = kt - 8
                                tk.mm(psc, ckT[es_, hp, a * 128:(a + 1) * 128], qn[es_, hp, qsl])
                            pscs[it] = psc

                        def stB(it):
                            ex = exb[len(exs) % len(exb)]
                            tk.act(ex, pscs.pop(it), AF.Exp)
                            exs[it] = ex

                        def stC(it):
                            hp, Q, e, kt = it
                            h = 2 * hp + e
                            es_ = slice(e * 64, (e + 1) * 64)
                            qsl = slice(Q * 512, (Q + 1) * 512)
                            if e == 0 and kt == kts[Q][0]:
                                pods[(hp, Q)] = (self.hold(), self.hold())
                            po, pd = pods[(hp, Q)]
                            vv = vn[:, kt, h * 64:(h + 1) * 64] if kt < 8 else cvt[:, kt - 8, h * 64:(h + 1) * 64]
                            ex = exs[it]
                            tk.mm(po[es_, :], vv, ex, start=(kt == kts[Q][0]), stop=(kt == 11), inc=True)
                            tk.mm(pd[es_, :], c["ones1"][:, 0:64], ex, start=(kt == kts[Q][0]), stop=(kt == 11), inc=True)
                            if e == 1 and kt == 11:
                                tk.recip(rden, pd)
                                tk.tt("dve", mixT[:, 4 + hp, qsl], po, rden, ALU.mult)
                                self.release(po)
                                self.release(pd)

                        for n in range(len(items) + LA):
                            if n < len(items):
                                stA(items[n])
                            if n >= LA:
                                stB(items[n - LA])
                                stC(items[n - LA])
                tk.barrier()
            with ExitStack() as os_:
                if lat and l == 1:
                    sel = c["sel"]
                    xo = self.xo
                    yTo = tk.sb(os_, "yTo", [128, 8, 256])
                    for r in range(4):
                        msl = mixT[:, :, r * 256:(r + 1) * 256]
                        xsl = self.xT[r // 2][:, :, (r % 2) * 256:(r % 2 + 1) * 256]
                        msl = mixT[:, 0:4, r * 256:(r + 1) * 256]
                        if r == 0:
                            tk.ts("dve", mixo[:, 0:4, :], msl, sel[:, 0:1], None, op0=ALU.mult)
                            tk.ts("dve", xo, xsl, sel[:, 0:1], None, op0=ALU.mult)
                        else:
                            tk.stt("dve", mixo[:, 0:4, :], msl, sel[:, r:r + 1], mixo[:, 0:4, :], ALU.mult, ALU.add)
                            tk.stt("dve", xo, xsl, sel[:, r:r + 1], xo, ALU.mult, ALU.add)
                    ev = 0
                    for b in range(2):
                        s3 = self.wnext(("wout", g, l, b))
                        for mm_ in range(4):
                            m = b * 4 + mm_
                            p = self.bank()
                            for k in range(8):
                                tk.mm(p[:, 0:256], s3[:, k, mm_ * 128:(mm_ + 1) * 128], mixo[:, k, :],
                                      start=(k == 0), stop=(k == 7))
                            tk.copy("act" if ev % 2 else "dve", yTo[:, m, :], p[:, 0:256])
                            ev += 1
                    self.boundary([yTo], [(xo, None, 256)], cf, 1, nxt)
                else:
                    yT = [tk.sb(os_, f"yTm{t}", [128, 8, 512]) for t in range(ntl)]
                    ev = 0
                    for b in range(2):
                        s3 = self.wnext(("wout", g, l, b))
                        for mm_ in range(4):
                            m = b * 4 + mm_
                            for t in range(ntl):
                                p = self.bank()
                                for k in range(8):
                                    tk.mm(p, s3[:, k, mm_ * 128:(mm_ + 1) * 128], mixT[:, k, t * 512:(t + 1) * 512],
                                          start=(k == 0), stop=(k == 7))
                                tk.copy("act" if ev % 2 else "dve", yT[t][:, m, :], p)
                                ev += 1
                    self.boundary(yT, [(self.xT[t], hT[t], 512) for t in range(ntl)], cf, 1, nxt)
                tk.barrier()


_NC_CACHE = {}


def build_nc():
    if "nc" not in _NC_CACHE:
        nc = bass.Bass("TRN2", target_bir_lowering=False)
        Kern(nc).build()
        _NC_CACHE["nc"] = nc
    return _NC_CACHE["nc"]


def kernel(x_prompt, x_sample, cache_k, cache_v, state_gla, c, c_ctx, w_mod, b_mod, norm_g,
           ffn_w_in, ffn_w_out, w_in, gla_wa2, gla_ba, gla_norm_g, nat_rpb, w_out):
    f = lambda a: np.ascontiguousarray(np.asarray(a, dtype=np.float32))
    x_prompt, x_sample, cache_k, cache_v, state_gla = map(f, (x_prompt, x_sample, cache_k, cache_v, state_gla))
    c, c_ctx = f(c), f(c_ctx)
    consts = make_consts()
    shared = {
        "w_mod": f(w_mod), "b_mod": f(b_mod).reshape(2, 72, 128), "norm_g": f(norm_g).reshape(96, 128),
        "ffn_w_in": f(ffn_w_in), "ffn_w_out": f(ffn_w_out), "w_in": f(w_in), "gla_wa2": f(gla_wa2),
        "gla_ba": f(gla_ba).reshape(2, 2, 1, 256), "gla_norm_g": f(gla_norm_g), "nat_rpb": f(nat_rpb),
        "w_out": f(w_out),
    }
    for k, v in consts.items():
        shared["c_" + k] = v
    in_maps = []
    cores = [int(x) for x in os.environ.get("KCORES", "0,1,2,3,4,5,6,7").split(",")]
    for core in cores:
        b = core // 4
        m = dict(shared)
        m["xp"] = x_prompt[2 * core:2 * core + 2].reshape(512, D)
        m["xs"] = x_sample[b]
        m["ck"] = cache_k[b].reshape(2, 512, 512)
        m["cv"] = cache_v[b].reshape(2, 512, 512)
        m["sg"] = state_gla[b].reshape(2, 2, 256, 128)
        m["cvec"] = np.ascontiguousarray(np.stack([c_ctx, c[b]])).reshape(16, 128)
        selv = np.zeros((128, 4), np.float32)
        selv[:, core % 4] = 1.0
        m["sel"] = selv
        m["pen_own"] = np.ascontiguousarray(consts["pen"][:, (core % 4) * 256:(core % 4 + 1) * 256])
        in_maps.append(m)
    nc = build_nc()
    res = run_bass_kernel_spmd(nc, in_maps, core_ids=list(range(len(cores))))
    R = res.results
    if len(cores) != 8:
        return {cores[i]: R[i] for i in range(len(cores))}
    y_prompt = np.stack([R[i]["yp"].reshape(2, 256, D) for i in range(8)]).reshape(16, 256, D)
    y_sample = np.stack([R[i]["ys"] for i in range(8)]).reshape(2, 1024, D)
    nk = np.stack([R[i]["nk"] for i in range(8)]).reshape(16, 2, 256, 8, 64)
    nv = np.stack([R[i]["nv"] for i in range(8)]).reshape(16, 2, 256, 8, 64)
    ns = np.stack([R[i]["ns"] for i in range(8)]).reshape(16, 2, 2, 4, 64, 128)
    return (y_prompt.astype(np.float32), y_sample.astype(np.float32), nk.astype(np.float32),
            nv.astype(np.float32), ns.astype(np.float32))
```
